# Optimizing a Trainium2 kernel written in Bass

```python
import jax, jax.numpy as jnp
from jax import lax
import numpy as np

D_MODEL = 1024
BATCH = 4
SEQ = 8192
DEPTH = 1
DEC_BATCH = 32
DEC_SEQ = 32
PAST_LEN = 1024

CHUNK = 64
HG_HEADS = 8
HG_DK = 128
HG_DV = 128
HG_WIDTH = HG_HEADS * HG_DV
POOL_WINDOWS = (2, 4, 8, 16)
POOL_GROUPS = 4
POOL_GC = 128
POOL_WIDTH = POOL_GROUPS * POOL_GC
POOL_BUF = 15
D_FF = -(-8 * D_MODEL // (3 * 256)) * 256
PLE_DIM = 256
IN_COLS = 4 * HG_WIDTH + POOL_WIDTH + 2 * D_MODEL
ALPHA = (2 * DEPTH) ** 0.25
BETA = (8 * DEPTH) ** -0.25
LN_EPS = 1e-5
RMS_EPS = 1e-6

kernel_name = "hgrn2_pool_gated_streaming_encoder_step"


def _layer_norm(x, g, b):
    xf = x.astype(jnp.float32)
    mu = jnp.mean(xf, axis=-1, keepdims=True)
    var = jnp.mean(jnp.square(xf - mu), axis=-1, keepdims=True)
    y = (xf - mu) * lax.rsqrt(var + LN_EPS) * g.astype(jnp.float32) + b.astype(jnp.float32)
    return y.astype(x.dtype)


def _hgrn_chunk(S0, q, logf, k, v):
    c = q.shape[2]
    b = jnp.cumsum(logf, axis=2)
    causal = jnp.tril(jnp.ones((c, c), dtype=bool))[:, :, None]
    decay = jnp.exp(jnp.where(causal, b[:, :, :, None, :] - b[:, :, None, :, :], -jnp.inf))
    scores = jnp.einsum("bhtk,bhtsk,bhsk->bhts", q, decay, k)
    o = (jnp.einsum("bhts,bhsv->bhtv", scores, v)
         + jnp.einsum("bhtk,bhkv->bhtv", q * jnp.exp(b), S0))
    b_last = b[:, :, -1:, :]
    S = (jnp.exp(b_last[:, :, 0, :, None]) * S0
         + jnp.einsum("bhsk,bhsv->bhkv", k * jnp.exp(b_last - b), v))
    return S, o


def _hgrn_recurrence(q, logf, k, v, S0):
    bsz, t, h, _ = q.shape
    cs = min(t, CHUNK)
    nc = t // cs

    def to_chunks(a):
        return a.reshape(bsz, nc, cs, h, a.shape[-1]).transpose(1, 0, 3, 2, 4)

    def step(S, inp):
        return _hgrn_chunk(S, *inp)

    S, o = lax.scan(step, S0, (to_chunks(q), to_chunks(logf), to_chunks(k), to_chunks(v)))
    o = o.transpose(1, 0, 3, 2, 4).reshape(bsz, t, h, HG_DV)
    return S, o


def _multiscale_pool(v, buf, offset):
    bsz, t, c = v.shape
    ext = jnp.concatenate([buf, v], axis=1)
    csum = jnp.concatenate([jnp.zeros((bsz, 1, c), jnp.float32), jnp.cumsum(ext, axis=1)], axis=1)
    pos = offset + jnp.arange(t)
    start = POOL_BUF + 1
    groups = []
    for j, w in enumerate(POOL_WINDOWS):
        lo, hi = j * POOL_GC, (j + 1) * POOL_GC
        win = csum[:, start:start + t, lo:hi] - csum[:, start - w:start - w + t, lo:hi]
        cnt = jnp.minimum(pos + 1, w).astype(jnp.float32)
        groups.append(win / cnt[None, :, None])
    pooled = jnp.concatenate(groups, axis=-1) - v
    return pooled, ext[:, -POOL_BUF:]


def _trunk_layer(x, p, S0, buf, offset, lb, w_in, hg_g, w_a, w_pm, p_scale, w_b, w_o,
                 ln1_g, ln1_b, w_up, w_down, w_pp, w_pg, ln2_g, ln2_b):
    bsz, t, _ = x.shape
    proj = x @ w_in
    splits = [HG_WIDTH, 2 * HG_WIDTH, 3 * HG_WIDTH, 4 * HG_WIDTH,
              4 * HG_WIDTH + POOL_WIDTH, 4 * HG_WIDTH + POOL_WIDTH + D_MODEL]
    q, fr, iv, g, v, ga, gb = jnp.split(proj, splits, axis=-1)

    lbh = lb.reshape(HG_HEADS, HG_DK)
    f = lbh + (1.0 - lbh) * jax.nn.sigmoid(fr.astype(jnp.float32).reshape(bsz, t, HG_HEADS, HG_DK))
    logf = jnp.log(f)
    k = 1.0 - f
    qf = q.astype(jnp.float32).reshape(bsz, t, HG_HEADS, HG_DK)
    vf = iv.astype(jnp.float32).reshape(bsz, t, HG_HEADS, HG_DV)
    S, o = _hgrn_recurrence(qf, logf, k, vf, S0.astype(jnp.float32))
    o = o * lax.rsqrt(jnp.mean(jnp.square(o), axis=-1, keepdims=True) + RMS_EPS)
    o = o * hg_g.astype(jnp.float32).reshape(HG_HEADS, HG_DV)
    o = o.reshape(bsz, t, HG_WIDTH) * jax.nn.silu(g.astype(jnp.float32))
    ya = o.astype(x.dtype) @ w_a

    pooled, new_buf = _multiscale_pool(v.astype(jnp.float32), buf.astype(jnp.float32), offset)
    pooled = jnp.einsum("btgc,gcd->btgd", pooled.reshape(bsz, t, POOL_GROUPS, POOL_GC),
                        w_pm.astype(jnp.float32)).reshape(bsz, t, POOL_WIDTH)
    yb = (pooled * p_scale.astype(jnp.float32)).astype(x.dtype) @ w_b

    m = jax.nn.sigmoid(ga) * ya + jax.nn.sigmoid(gb) * yb
    x = _layer_norm(ALPHA * x + m @ w_o, ln1_g, ln1_b)

    gt, up = jnp.split(x @ w_up, [D_FF], axis=-1)
    ffn = (jax.nn.silu(gt) * up) @ w_down
    ple = jax.nn.sigmoid(x @ w_pg) * (p @ w_pp)
    x = _layer_norm(ALPHA * x + ffn + ple, ln2_g, ln2_b)
    return x, S.astype(S0.dtype), new_buf.astype(buf.dtype)


def _normal(key, shape, scale):
    return scale * jax.random.normal(key, shape, jnp.float32)


def setup_inputs(seed: int = 0) -> dict:
    key = jax.random.key(seed)
    ks = jax.random.split(key, 24)
    return {
        "x_prompt": _normal(ks[0], (BATCH, SEQ, D_MODEL), 1.0),
        "x_sample": _normal(ks[1], (DEC_BATCH, DEC_SEQ, D_MODEL), 1.0),
        "p_prompt": _normal(ks[2], (DEPTH, BATCH, SEQ, PLE_DIM), 1.0),
        "p_sample": _normal(ks[3], (DEPTH, DEC_BATCH, DEC_SEQ, PLE_DIM), 1.0),
        "state_hgrn": _normal(ks[4], (DEPTH, DEC_BATCH, HG_HEADS, HG_DK, HG_DV), 0.5),
        "state_pool": _normal(ks[5], (DEPTH, DEC_BATCH, POOL_BUF, POOL_WIDTH), 1.0),
        "ln_in_g": 1.0 + _normal(ks[6], (D_MODEL,), 0.02),
        "ln_in_b": _normal(ks[7], (D_MODEL,), 0.02),
        "lb_logits": _normal(ks[8], (DEPTH + 1, HG_WIDTH), 0.1),
        "w_in": _normal(ks[9], (DEPTH, D_MODEL, IN_COLS), D_MODEL ** -0.5),
        "hgrn_norm_g": 1.0 + _normal(ks[10], (DEPTH, HG_WIDTH), 0.02),
        "w_branch_a": _normal(ks[11], (DEPTH, HG_WIDTH, D_MODEL), BETA * HG_WIDTH ** -0.5),
        "w_pool_mix": _normal(ks[12], (DEPTH, POOL_GROUPS, POOL_GC, POOL_GC), POOL_GC ** -0.5),
        "pool_scale": 1.0 + _normal(ks[13], (DEPTH, POOL_WIDTH), 0.02),
        "w_branch_b": _normal(ks[14], (DEPTH, POOL_WIDTH, D_MODEL), BETA * POOL_WIDTH ** -0.5),
        "w_out": _normal(ks[15], (DEPTH, D_MODEL, D_MODEL), BETA * D_MODEL ** -0.5),
        "ln1_g": 1.0 + _normal(ks[16], (DEPTH, D_MODEL), 0.02),
        "ln1_b": _normal(ks[17], (DEPTH, D_MODEL), 0.02),
        "w_ffn_up": _normal(ks[18], (DEPTH, D_MODEL, 2 * D_FF), D_MODEL ** -0.5),
        "w_ffn_down": _normal(ks[19], (DEPTH, D_FF, D_MODEL), BETA * D_FF ** -0.5),
        "w_ple_proj": _normal(ks[20], (DEPTH, PLE_DIM, D_MODEL), BETA * PLE_DIM ** -0.5),
        "w_ple_gate": _normal(ks[21], (DEPTH, D_MODEL, D_MODEL), D_MODEL ** -0.5),
        "ln2_g": 1.0 + _normal(ks[22], (DEPTH, D_MODEL), 0.02),
        "ln2_b": _normal(ks[23], (DEPTH, D_MODEL), 0.02),
    }


def reference(x_prompt, x_sample, p_prompt, p_sample, state_hgrn, state_pool, ln_in_g, ln_in_b,
              lb_logits, w_in, hgrn_norm_g, w_branch_a, w_pool_mix, pool_scale, w_branch_b, w_out,
              ln1_g, ln1_b, w_ffn_up, w_ffn_down, w_ple_proj, w_ple_gate, ln2_g, ln2_b):
    lb_all = jnp.cumsum(jax.nn.softmax(lb_logits.astype(jnp.float32), axis=0), axis=0)[:DEPTH]
    xp = _layer_norm(x_prompt, ln_in_g, ln_in_b)
    xs = _layer_norm(x_sample, ln_in_g, ln_in_b)
    bp = xp.shape[0]
    hp_list, pp_list, hs_list, ps_list = [], [], [], []
    for i in range(DEPTH):
        params = (lb_all[i], w_in[i], hgrn_norm_g[i], w_branch_a[i], w_pool_mix[i], pool_scale[i],
                  w_branch_b[i], w_out[i], ln1_g[i], ln1_b[i], w_ffn_up[i], w_ffn_down[i],
                  w_ple_proj[i], w_ple_gate[i], ln2_g[i], ln2_b[i])
        S0p = jnp.zeros((bp, HG_HEADS, HG_DK, HG_DV), state_hgrn.dtype)
        buf0p = jnp.zeros((bp, POOL_BUF, POOL_WIDTH), state_pool.dtype)
        xp, Sp, bufp = _trunk_layer(xp, p_prompt[i], S0p, buf0p, 0, *params)
        xs, Ss, bufs = _trunk_layer(xs, p_sample[i], state_hgrn[i], state_pool[i], PAST_LEN, *params)
        hp_list.append(Sp)
        pp_list.append(bufp)
        hs_list.append(Ss)
        ps_list.append(bufs)
    new_state_hgrn_prompt = jnp.stack(hp_list, axis=0)
    new_state_pool_prompt = jnp.stack(pp_list, axis=0)
    new_state_hgrn_sample = jnp.stack(hs_list, axis=0)
    new_state_pool_sample = jnp.stack(ps_list, axis=0)
    return (xp, xs, new_state_hgrn_prompt, new_state_pool_prompt, new_state_hgrn_sample, new_state_pool_sample)
```

```python
import os
from contextlib import ExitStack
import numpy as np
import concourse.bass as bass
import concourse.mybir as mybir
from concourse.bass_utils import run_bass_kernel_spmd

F32 = mybir.dt.float32
BF16 = mybir.dt.bfloat16
I32 = mybir.dt.int32
AF = mybir.ActivationFunctionType
ALU = mybir.AluOpType

D = 1024
NCORE = 8
SEQ = 8192
HALF = 4096
TT = 512
DFF = 2816
NBF = 22
ALPHA = float(2.0 ** 0.25)
LN_EPS = 1e-5
RMS_EPS = 1e-6
NSLOT = 4
SLAB = 4096


class Tok:
    __slots__ = ("name", "w", "r")

    def __init__(self, name):
        self.name = name
        self.w = None
        self.r = {}


class Sched:
    ENG = ("pe", "act", "dve", "pool", "sp")

    def __init__(self, nc, stack):
        self.nc = nc
        self.stack = stack
        self.ops = {e: [] for e in self.ENG}
        self.semh = {}
        self.cnt = {}
        for e in self.ENG:
            self.semh[e] = stack.enter_context(nc.semaphore("s_" + e))
            self.cnt[e] = 0
        self.seen = {e: {} for e in self.ENG}
        self.pending = {e: [] for e in self.ENG}
        self.isdma = set()
        self.nwait = 0

    def dsem(self, name):
        if name not in self.semh:
            self.semh[name] = self.stack.enter_context(self.nc.semaphore("d_" + name))
            self.cnt[name] = 0
            self.isdma.add(name)
        return name

    def _wait(self, eng, key, val):
        if key.startswith("PEND:"):
            if key == "PEND:" + eng and eng == "pe":
                return
            raise RuntimeError("dependency on pending event %s from %s" % (key, eng))
        if key == "pe" and eng == "pe":
            return
        if key in self.isdma:
            val = max(val, self.cnt[key])
        if self.seen[eng].get(key, 0) >= val:
            return
        self.seen[eng][key] = val
        sem = self.semh[key]
        self.nwait += 1
        self.ops[eng].append(lambda e, sem=sem, val=val: e.wait_ge(sem, val))

    def _deps(self, eng, reads, writes):
        for t in reads:
            if t.w is not None:
                self._wait(eng, *t.w)
        for t in writes:
            if t.w is not None:
                self._wait(eng, *t.w)
            for k, v in list(t.r.items()):
                self._wait(eng, k, v)

    def op(self, eng, fn, r=(), w=(), inc=True):
        self._deps(eng, r, w)
        pend = self.pending[eng]
        for t in r:
            pend.append((t, 0))
        for t in w:
            pend.append((t, 1))
            if not inc:
                t.w = ("PEND:" + eng, 0)
                t.r = {}
        if not inc:
            for t in r:
                t.r["PEND:" + eng] = 0
            self.ops[eng].append(lambda e, fn=fn: fn(e))
            return
        self.cnt[eng] += 1
        v = self.cnt[eng]
        sem = self.semh[eng]
        self.ops[eng].append(lambda e, fn=fn, sem=sem: fn(e).then_inc(sem, 1))
        for t, isw in pend:
            if isw:
                t.w = (eng, v)
                t.r = {}
            else:
                t.r.pop("PEND:" + eng, None)
                t.r[eng] = v
        pend.clear()

    def dma(self, queue, out, in_, sem, r=(), w=(), **kw):
        self._deps(queue, r, w)
        self.dsem(sem)
        self.cnt[sem] += 16
        v = self.cnt[sem]
        h = self.semh[sem]
        self.ops[queue].append(
            lambda e, out=out, in_=in_, h=h, kw=kw: e.dma_start(out=out, in_=in_, **kw).then_inc(h, 16))
        for t in r:
            t.r[sem] = v
        for t in w:
            t.w = (sem, v)
            t.r = {}


def build_program(n_pre=8, n_main=8, do_sample=True):
    nc = bass.Bass("TRN2", target_bir_lowering=False)
    stack = ExitStack()

    def din(name, shape):
        return nc.dram_tensor(name, shape, F32, kind="ExternalInput").ap()

    def dout(name, shape):
        return nc.dram_tensor(name, shape, F32, kind="ExternalOutput").ap()

    RM = n_main * TT
    RP = max(n_pre, 1) * TT
    x_main = din("x_main", [RM, D])
    x_pre = din("x_pre", [RP, D])
    p_main = din("p_main", [RM, 256])
    x_s = din("x_s", [128, D])
    p_s = din("p_s", [128, 256])
    s_hgrn = din("s_hgrn", [4, 8, 128, 128])
    s_pool = din("s_pool", [4, 15, 512])
    meta = din("meta", [128, 2])
    ln_in_g = din("ln_in_g", [D]); ln_in_b = din("ln_in_b", [D])
    lb_logits = din("lb_logits", [2, D])
    w_in = din("w_in", [D, 6656])
    hg_g = din("hg_g", [D])
    w_a = din("w_a", [D, D])
    w_pm = din("w_pm", [4, 128, 128])
    p_scale = din("p_scale", [512])
    w_b = din("w_b", [512, D])
    w_o = din("w_o", [D, D])
    ln1_g = din("ln1_g", [D]); ln1_b = din("ln1_b", [D])
    w_up = din("w_up", [D, 2 * DFF])
    w_down = din("w_down", [DFF, D])
    w_pp = din("w_pp", [256, D])
    w_pg = din("w_pg", [D, D])
    ln2_g = din("ln2_g", [D]); ln2_b = din("ln2_b", [D])

    y_main = dout("y_main", [RM, D])
    y_s = dout("y_s", [128, D])
    S_out = dout("S_out", [8, 128, 128])
    pool_out = dout("pool_out", [15, 512])
    Ss_out = dout("Ss_out", [4, 8, 128, 128])
    pools_out = dout("pools_out", [4, 15, 512])

    slabs = []

    def add(name, pieces):
        slabs.append((name, pieces))
        return len(slabs) - 1

    def wcol(wap, c0):
        return [(wap, 0, 8, c0, 512, 0)]

    SL = {}
    SL["f0"] = add("f0", wcol(w_in, 1024)); SL["q0"] = add("q0", wcol(w_in, 0))
    SL["f1"] = add("f1", wcol(w_in, 1536)); SL["q1"] = add("q1", wcol(w_in, 512))
    SL["i0"] = add("i0", wcol(w_in, 2048)); SL["i1"] = add("i1", wcol(w_in, 2560))
    SL["g0"] = add("g0", wcol(w_in, 3072)); SL["g1"] = add("g1", wcol(w_in, 3584))
    SL["v"] = add("v", wcol(w_in, 4096))
    for h in range(2):
        SL["ga%d" % h] = add("ga%d" % h, wcol(w_in, 4608 + 512 * h))
        SL["gb%d" % h] = add("gb%d" % h, wcol(w_in, 5632 + 512 * h))
        SL["wb%d" % h] = add("wb%d" % h, [(w_b, 0, 4, 512 * h, 512, 0)])
        SL["wa%d" % h] = add("wa%d" % h, wcol(w_a, 512 * h))
    SL["wo0"] = add("wo0", wcol(w_o, 0)); SL["wo1"] = add("wo1", wcol(w_o, 512))
    for s in range(11):
        SL["up%d" % s] = add("up%d" % s, [(w_up, 0, 8, 256 * s, 256, 0), (w_up, 0, 8, DFF + 256 * s, 256, 2048)])
    for h in range(2):
        SL["pg%d" % h] = add("pg%d" % h, wcol(w_pg, 512 * h))
        SL["pp%d" % h] = add("pp%d" % h, [(w_pp, 0, 2, 512 * h, 512, 0)])
        for kg in range(3):
            nk = 8 if kg < 2 else 6
            SL["dn%d%d" % (h, kg)] = add("dn%d%d" % (h, kg), [(w_down, 8 * kg, nk, 512 * h, 512, 0)])
    NSLAB = len(slabs)
    scratch = nc.dram_tensor("wscr", [NSLAB, 128, SLAB], BF16, kind="Internal").ap()

    pre_seq = ["f0", "f1", "i0", "i1"]
    main_seq = [s[0] for s in slabs]
    slab_seq = []
    for t in range(n_pre):
        slab_seq += pre_seq + (["v"] if t == n_pre - 1 else [])
    for t in range(n_main + (1 if do_sample else 0)):
        slab_seq += main_seq

    R = Sched(nc, stack)

    def sb(name, shape, dt):
        return stack.enter_context(nc.sbuf_tensor(name, shape, dt))

    wring = sb("wring", [128, NSLOT, SLAB], BF16); wring_k = [Tok("wr%d" % i) for i in range(NSLOT)]
    xt = sb("xt", [128, 8, D], F32); xt_k = [Tok("xt%d" % i) for i in range(8)]
    pt = sb("pt", [128, 4, 256], F32); pt_k = [Tok("pt%d" % i) for i in range(4)]
    xb = sb("xb", [128, 2, D], BF16); xb_k = [Tok("xb0"), Tok("xb1")]
    pb = sb("pb", [128, 256], BF16); pb_k = Tok("pb")
    xT = sb("xT", [128, 8, TT], BF16); xT_k = [Tok("xT%d" % i) for i in range(4)]
    pT = sb("pT", [128, 2, TT], BF16); pT_k = [Tok("pT%d" % i) for i in range(4)]
    big = sb("big", [128, 24, TT], BF16); big_k = [Tok("big%d" % i) for i in range(24)]
    Eb = sb("Eb", [128, 4, TT], F32); Eb_k = [Tok("Eb%d" % i) for i in range(4)]
    NTMP = 7
    tmp = sb("tmp", [128, NTMP, 528], F32); tmp_k = [Tok("tmp%d" % i) for i in range(NTMP)]
    vtm = sb("vtm", [128, 4, D], BF16); vtm_k = [Tok("vtm%d" % i) for i in range(4)]
    sgT = sb("sgT", [128, 8, TT], BF16); sgT_k = [Tok("sgT%d" % i) for i in range(8)]
    oT = sb("oT", [128, 8, TT], BF16); oT_k = [Tok("oT%d" % i) for i in range(4)]
    gaS = sb("gaS", [128, 4, TT], BF16); gaS_k = [Tok("gaS%d" % i) for i in range(4)]
    gbS = sb("gbS", [128, 4, TT], BF16); gbS_k = [Tok("gbS%d" % i) for i in range(4)]
    EXTW = 15 + TT
    pooledT = sb("pooledT", [128, 4, TT], BF16); pooled_k = [Tok("pooled%d" % i) for i in range(4)]
    pmT = sb("pmT", [128, 4, TT], BF16); pmT_k = [Tok("pmT%d" % i) for i in range(4)]
    carry = sb("carry", [128, 4, 15], F32); carry_k = [Tok("carry%d" % i) for i in range(4)]
    spre = sb("spre", [128, 4, 4, 15], F32); spre_k = Tok("spre")
    Sst = sb("Sst", [128, 2, 8, 128], F32); Sst_k = [[Tok("S%d_%d" % (a, h)) for h in range(8)] for a in range(2)]
    NSB = 4
    Sb = sb("Sb", [128, NSB, 8, 128], BF16); Sb_k = [Tok("Sb%d" % i) for i in range(NSB)]
    khat = sb("khat", [128, 1, 8, 128], BF16); khat_k = [Tok("khat0"), Tok("khat0")]
    khat_k[1] = khat_k[0]
    sq = sb("sq", [128, 1, TT], BF16); sq_k = [Tok("sq0")] * 2
    elast = sb("elast", [128, 8, 8], F32); elast_k = [Tok("el%d" % i) for i in range(8)]
    lnG = sb("lnG", [128, D], F32); lnG_k = Tok("lnG")
    lnB = sb("lnB", [128, D], F32); lnB_k = Tok("lnB")
    wpm = sb("wpm", [128, 4, 128], BF16); wpm_k = Tok("wpm")
    NST = 4
    st = sb("st", [128, NST, 12], F32); mv = sb("mv", [128, NST, 2], F32); rs = sb("rs", [128, NST, 4], F32)
    st_k = [Tok("st%d" % i) for i in range(NST)]
    cst = sb("cst", [128, 8], F32); cst_k = Tok("cst")
    identb = sb("identb", [128, 128], BF16); identf = sb("identf", [128, 128], F32); onesb = sb("onesb", [128, 128], BF16)
    onesf = sb("onesf", [128, 128], F32)
    ident_k = Tok("ident")
    mask64 = sb("mask64", [128, 128], BF16); mask32 = sb("mask32", [128, 128], BF16); mask_k = Tok("mask")
    rm64 = sb("rm64", [128, TT], BF16); rm32 = sb("rm32", [128, TT], BF16); rm_k = Tok("rm")
    lbt = sb("lbt", [128, 2, 8], F32)
    oml = sb("oml", [128, 8], F32); noml = sb("noml", [128, 8], F32); lb_k = Tok("lb")
    hgT = sb("hgT", [128, 8], F32); pscT = sb("pscT", [128, 4], F32); vec_k = Tok("vec")
    metat = sb("metat", [128, 2], F32); meta_k = Tok("meta")
    invc = sb("invc", [128, 4, 16], F32); invc_k = Tok("invc")
    iot = sb("iot", [128, 16], I32); iof = sb("iof", [128, 16], F32)

    psum = stack.enter_context(nc.psum_tensor("psum", [128, 8, 512], F32))
    ps_k = [Tok("ps%d" % i) for i in range(8)]
    state = {"ps": 0, "tmp": 0, "slab": 0, "st": 0, "sbv": 0, "ln": None}

    ps_reserved = set()

    def PS():
        while True:
            i = state["ps"] % 8
            state["ps"] += 1
            if i not in ps_reserved:
                return psum[:, i, :], ps_k[i]

    pspools = {"hA": ([0, 1, 2, 3], [0]), "hO": ([4, 5, 6, 7], [0])}

    def PSP(name):
        banks, c = pspools[name]
        i = banks[c[0] % len(banks)]
        c[0] += 1
        if name == "hO":
            ps_reserved.add(i)
        return psum[:, i, :], ps_k[i]

    def TMPW():
        i = state["tmp"] % NTMP
        state["tmp"] += 1
        return tmp[:, i, :], tmp_k[i]

    def TMP():
        i = state["tmp"] % NTMP
        state["tmp"] += 1
        return tmp[:, i, 0:TT], tmp_k[i]

    scr_k = [Tok("scr%d" % i) for i in range(NSLAB)]
    cast_done = set()

    def cast_slab(si):
        if si in cast_done:
            return
        cast_done.add(si)
        name, pieces = slabs[si]
        for (src, k0, nk, c0, ncols, off) in pieces:
            R.dma("pool", scratch[si][:, off:off + nk * ncols].rearrange("p (k c) -> p k c", k=nk),
                  src[k0 * 128:(k0 + nk) * 128, c0:c0 + ncols].rearrange("(k p) c -> p k c", p=128),
                  sem="scr%d" % si, w=[scr_k[si]])

    cast_order = [SL[n] for n in (pre_seq + ["v"])] + [i for i in range(NSLAB)]

    def cast_some(n):
        k = 0
        for si in cast_order:
            if k >= n:
                break
            if si not in cast_done:
                cast_slab(si)
                k += 1

    cast_some(5 if n_pre > 0 else NSLAB)

    slab_pos = {"i": 0, "issued": 0}

    def issue_loads(upto):
        while slab_pos["issued"] < min(upto, len(slab_seq)):
            j = slab_pos["issued"]
            si = SL[slab_seq[j]]
            cast_slab(si)
            slot = j % NSLOT
            used = max(off + nk * ncols for (_, _, nk, _, ncols, off) in slabs[si][1])
            R.dma("sp", wring[:, slot, 0:used], scratch[si][:, 0:used], sem="wr%d" % slot, r=[scr_k[si]], w=[wring_k[slot]])
            slab_pos["issued"] += 1

    def next_slab(name):
        i = slab_pos["i"]
        assert slab_seq[i] == name, (slab_seq[i], name, i)
        issue_loads(i + 3)
        slab_pos["i"] += 1
        slot = i % NSLOT
        return wring[:, slot, :], wring_k[slot]

    R.op("pool", lambda e: e.memset(cst[:, 0:1], LN_EPS), w=[cst_k])
    R.op("pool", lambda e: e.memset(cst[:, 1:2], 1.0), w=[cst_k])
    R.op("pool", lambda e: e.memset(cst[:, 2:3], RMS_EPS), w=[cst_k])
    R.op("pool", lambda e: e.memset(onesf[:], 1.0), w=[ident_k])
    R.op("pool", lambda e: e.memset(onesb[:], 1.0 / 128.0), w=[ident_k])
    R.op("pool", lambda e: e.affine_select(out=identf[:], in_=onesf[:], pattern=[[-1, 128]], compare_op=ALU.is_equal,
                                           fill=0.0, base=0, channel_multiplier=1), r=[ident_k], w=[ident_k])
    R.op("pool", lambda e: e.tensor_copy(out=identb[:], in_=identf[:]), r=[ident_k], w=[ident_k])
    for mk, cs in ((mask64, 64), (mask32, 32)):
        R.op("pool", lambda e, mk=mk: e.affine_select(out=mk[:], in_=onesf[:], pattern=[[1, 128]], compare_op=ALU.is_ge,
                                                      fill=0.0, base=0, channel_multiplier=-1), r=[ident_k], w=[mask_k])
        for i in range(128 // cs - 1):
            R.op("pool", lambda e, mk=mk, i=i, cs=cs: e.memset(mk[cs * i:cs * (i + 1), cs * (i + 1):128], 0.0), w=[mask_k])
    for rm, cs in ((rm64, 64), (rm32, 32)):
        R.op("pool", lambda e, rm=rm: e.memset(rm[:], 1.0), w=[rm_k])
        R.op("pool", lambda e, rm=rm, cs=cs: e.memset(rm[:].rearrange("p (c s) -> p c s", s=cs)[:, :, 0:1], 0.0), w=[rm_k])
    R.dma("act", lbt[:], lb_logits.rearrange("r (h p) -> p r h", p=128), sem="c0", w=[lb_k], allow_slow_non_contiguous=True)
    R.dma("act", hgT[:], hg_g.rearrange("(h p) -> p h", p=128), sem="c1", w=[vec_k], allow_slow_non_contiguous=True)
    R.dma("act", pscT[:], p_scale.rearrange("(h p) -> p h", p=128), sem="c1", w=[vec_k], allow_slow_non_contiguous=True)
    R.dma("act", metat[:], meta, sem="c2", w=[meta_k])
    R.dma("pool", wpm[:], w_pm.rearrange("g c d -> c g d"), sem="c3", w=[wpm_k])
    R.op("dve", lambda e: e.tensor_tensor(out=oml[:], in0=lbt[:, 0, :], in1=lbt[:, 1, :], op=ALU.subtract), r=[lb_k], w=[lb_k])
    R.op("act", lambda e: e.activation(out=oml[:], in_=oml[:], func=AF.Exp), r=[lb_k], w=[lb_k])
    R.op("act", lambda e: e.activation(out=oml[:], in_=oml[:], func=AF.Ln, bias=cst[:, 1:2], scale=1.0), r=[lb_k, cst_k], w=[lb_k])
    R.op("act", lambda e: e.activation(out=oml[:], in_=oml[:], func=AF.Exp, scale=-1.0), r=[lb_k], w=[lb_k])
    R.op("dve", lambda e: e.tensor_scalar(out=noml[:], in0=oml[:], scalar1=-1.0, scalar2=None, op0=ALU.mult), r=[lb_k], w=[lb_k])
    R.op("pool", lambda e: e.iota(out=iot[:], pattern=[[1, 16]], base=1, channel_multiplier=0), w=[invc_k])
    R.op("pool", lambda e: e.tensor_copy(out=iof[:], in_=iot[:]), r=[invc_k], w=[invc_k])
    for g in range(4):
        R.op("dve", lambda e, g=g: e.tensor_scalar(out=invc[:, g, :], in0=iof[:], scalar1=metat[:, 1:2], scalar2=float(2 << g),
                                                   op0=ALU.add, op1=ALU.min), r=[invc_k, meta_k], w=[invc_k])
    R.op("dve", lambda e: e.reciprocal(out=invc[:], in_=invc[:]), r=[invc_k], w=[invc_k])
    for h in range(8):
        R.op("pool", lambda e, h=h: e.memset(Sst[:, 0, h, :], 0.0), w=[Sst_k[0][h]])
    for g in range(4):
        R.op("pool", lambda e, g=g: e.memset(carry[:, g, :], 0.0), w=[carry_k[g]])

    for i in range(NTMP):
        R.op("pool", lambda e, i=i: e.memset(tmp[:, i, :], 0.0), w=[tmp_k[i]])

    if do_sample:
        for j in range(4):
            R.dma("act", Eb[0:15, j, :], s_pool[j], sem="c5", w=[Eb_k[j]])
        pst, pk = PS()
        for j in range(4):
            for g in range(4):
                last = (j == 3 and g == 3)
                R.op("pe", lambda e, j=j, g=g, pst=pst: e.transpose(out=pst[:, (g * 4 + j) * 15:(g * 4 + j) * 15 + 15],
                                                           in_=Eb[0:15, j, g * 128:(g + 1) * 128], identity=identf[0:15, 0:15]),
                     r=[Eb_k[j], ident_k], w=[pk], inc=last)
        R.op("dve", lambda e, pst=pst: e.tensor_copy(out=spre[:].rearrange("p g j r -> p (g j r)"), in_=pst[:, 0:240]), r=[pk], w=[spre_k])

    LNP = {"in": (ln_in_g, ln_in_b), "1": (ln1_g, ln1_b), "2": (ln2_g, ln2_b)}

    def load_ln(name):
        if state["ln"] == name:
            return
        state["ln"] = name
        g_ap, b_ap = LNP[name]
        R.dma("pool", lnG[:], g_ap.partition_broadcast(128), sem="lng", w=[lnG_k])
        R.dma("pool", lnB[:], b_ap.partition_broadcast(128), sem="lnb", w=[lnB_k])

    ln_st = {}

    def ln_stats(slot):
        x = xt[:, slot, :]
        k = xt_k[slot]
        si = state["st"] % NST
        state["st"] += 1
        sk = st_k[si]
        R.op("dve", lambda e: e.bn_stats(out=st[:, si, 0:6], in_=xt[:, slot, 0:512]), r=[k], w=[sk])
        R.op("dve", lambda e: e.bn_stats(out=st[:, si, 6:12], in_=xt[:, slot, 512:1024]), r=[k], w=[sk])
        R.op("dve", lambda e: e.bn_aggr(out=mv[:, si, :], in_=st[:, si, :]), r=[sk], w=[sk])
        R.op("act", lambda e: e.activation(out=rs[:, si, 0:1], in_=mv[:, si, 1:2], func=AF.Ln, bias=cst[:, 0:1], scale=1.0),
             r=[sk, cst_k], w=[sk])
        R.op("act", lambda e: e.activation(out=rs[:, si, 1:2], in_=rs[:, si, 0:1], func=AF.Exp, scale=-0.5), r=[sk], w=[sk])
        R.op("dve", lambda e: e.scalar_tensor_tensor(out=rs[:, si, 2:3], in0=mv[:, si, 0:1], scalar=-1.0, in1=rs[:, si, 1:2],
                                                     op0=ALU.mult, op1=ALU.mult), r=[sk], w=[sk])
        R.op("act", lambda e: e.activation(out=x, in_=x, func=AF.Identity, bias=rs[:, si, 2:3], scale=rs[:, si, 1:2]),
             r=[sk, k], w=[k])

    def ln_affine(slot, eng="pool"):
        x = xt[:, slot, :]
        k = xt_k[slot]
        R.op(eng, lambda e: e.tensor_tensor(out=x, in0=x, in1=lnG[:], op=ALU.mult), r=[k, lnG_k], w=[k])
        R.op(eng, lambda e: e.tensor_tensor(out=x, in0=x, in1=lnB[:], op=ALU.add), r=[k, lnB_k], w=[k])

    def ln_cast(slot, bf_slot):
        R.op("act", lambda e: e.activation(out=xb[:, bf_slot, :], in_=xt[:, slot, :], func=AF.Copy), r=[xt_k[slot]], w=[xb_k[bf_slot]])

    def transpose_to_xT(sub, bf_slot):
        pst, pk = PS()
        psb = pst.bitcast(BF16).rearrange("p (b t) -> p b t", b=8)
        for blk in range(8):
            R.op("pe", lambda e, blk=blk: e.transpose(out=psb[:, blk, :], in_=xb[:, bf_slot, blk * 128:(blk + 1) * 128],
                                                      identity=identb[:]), r=[xb_k[bf_slot], ident_k], w=[pk], inc=(blk == 7))
        R.op("dve", lambda e: e.tensor_copy(out=xT[:, :, sub * 128:(sub + 1) * 128], in_=psb), r=[pk], w=[xT_k[sub]])

    def fm_proj(slab, col0, rhsT, rhs_k, ntok, nk=8):
        sl, slk = slab
        pst, pk = PS()
        for k in range(nk):
            R.op("pe", lambda e, k=k: e.matmul(pst[:, 0:ntok], lhsT=sl[:, k * 512 + col0:k * 512 + col0 + 128],
                                               rhs=rhsT[:, k, 0:ntok], start=(k == 0), stop=(k == nk - 1)),
                 r=[slk] + rhs_k, w=[pk], inc=(k == nk - 1))
        return pst[:, 0:ntok], pk


    def fblock_a(blk, ps_ap, pk, ntok, CS, main):
        rm = rm64 if CS == 64 else rm32
        t0, k0 = TMP(); t1, k1 = TMP(); t2, k2 = TMP()
        a0 = t0[:, 0:ntok]; a1 = t1[:, 0:ntok]; a2 = t2[:, 0:ntok]
        R.op("act", lambda e: e.activation(out=a0, in_=ps_ap, func=AF.Exp), r=[pk], w=[k0])
        R.op("act", lambda e: e.activation(out=a0, in_=a0, func=AF.Ln, bias=cst[:, 1:2], scale=1.0), r=[k0, cst_k], w=[k0])
        R.op("act", lambda e: e.activation(out=a0, in_=a0, func=AF.Exp, scale=-1.0), r=[k0], w=[k0])
        R.op("act", lambda e: e.activation(out=a1, in_=a0, func=AF.Ln, bias=cst[:, 1:2], scale=noml[:, blk:blk + 1]),
             r=[k0, cst_k, lb_k], w=[k1])
        R.op("dve", lambda e: e.tensor_scalar(out=a0, in0=a0, scalar1=oml[:, blk:blk + 1], scalar2=None, op0=ALU.mult),
             r=[k0, lb_k], w=[k0])
        R.op("dve", lambda e: e.tensor_tensor_scan(out=a2, data0=rm[:, 0:ntok], data1=a1, initial=0.0, op0=ALU.mult, op1=ALU.add),
             r=[k1, rm_k], w=[k2])
        return (blk, a0, a1, a2, k0, k1, k2, ntok, CS, main)

    def fblock_b(ctx):
        blk, a0, a1, a2, k0, k1, k2, ntok, CS, main = ctx
        NCH = ntok // CS
        b3 = a2.rearrange("p (c s) -> p c s", s=CS)
        R.op("act", lambda e: e.activation(out=elast[:, blk, 0:NCH], in_=b3[:, :, CS - 1], func=AF.Exp), r=[k2], w=[elast_k[blk]])
        if main:
            R.op("act", lambda e: e.activation(out=Eb[:, blk % 4, 0:ntok], in_=a2, func=AF.Exp), r=[k2], w=[Eb_k[blk % 4]])
        R.op("act", lambda e: e.activation(out=a1, in_=a2, func=AF.Exp, scale=-1.0), r=[k2, k1], w=[k1])
        pe_ = "pool" if main else "dve"
        R.op(pe_, lambda e: e.tensor_tensor(out=a1, in0=a0, in1=a1, op=ALU.mult), r=[k0, k1], w=[k1])
        if main:
            R.op("pool", lambda e: e.tensor_copy(out=big[:, 8 + blk, 0:ntok], in_=a1), r=[k1], w=[big_k[8 + blk]])
        R.op(pe_, lambda e: e.tensor_tensor(out=big[:, 16 + blk, 0:ntok].rearrange("p (c s) -> p c s", s=CS),
                                               in0=a1.rearrange("p (c s) -> p c s", s=CS),
                                               in1=elast[:, blk, 0:NCH].unsqueeze(2).broadcast_to([128, NCH, CS]), op=ALU.mult),
             r=[k1, elast_k[blk]], w=[big_k[16 + blk]])

    def qblock(blk, ps_ap, pk, ntok):
        R.op("dve", lambda e: e.tensor_tensor(out=big[:, blk, 0:ntok], in0=ps_ap, in1=Eb[:, blk % 4, 0:ntok], op=ALU.mult),
             r=[pk, Eb_k[blk % 4]], w=[big_k[blk]])

    def pool_group(g, ps_ap, pk, ntok, nseq, sample, first_main, save_carry):
        L = ntok // nseq
        E = 15 + L
        w = 2 << g
        ex, exk = TMPW(); e2_, e2k = TMPW(); e3_, e3k = TMPW()
        ev = ex[:, 0:nseq * E].rearrange("p (j e) -> p j e", j=nseq)
        e2 = e2_[:, 0:nseq * E].rearrange("p (j e) -> p j e", j=nseq)
        e3 = e3_[:, 0:nseq * E].rearrange("p (j e) -> p j e", j=nseq)
        R.op("act", lambda e: e.activation(out=ev[:, :, 15:E], in_=ps_ap.rearrange("p (j l) -> p j l", j=nseq), func=AF.Copy),
             r=[pk], w=[exk])
        if sample:
            R.op("dve", lambda e: e.tensor_copy(out=ev[:, :, 0:15], in_=spre[:, g, :, :]), r=[spre_k], w=[exk])
        else:
            R.op("dve", lambda e: e.tensor_copy(out=ev[:, 0, 0:15], in_=carry[:, g, :]), r=[carry_k[g]], w=[exk])
        src, sk = ev, exk
        for si_ in range(g + 1):
            sh = 1 << si_
            dst, dk = (e2, e2k) if si_ % 2 == 0 else (e3, e3k)
            R.op("dve", lambda e, dst=dst, src=src, sh=sh: e.tensor_tensor(out=dst[:, :, sh:E], in0=src[:, :, sh:E],
                                                                           in1=src[:, :, 0:E - sh], op=ALU.add),
                 r=[sk], w=[dk])
            src, sk = dst, dk
        outv = pooledT[:, g, 0:ntok].rearrange("p (j l) -> p j l", j=nseq)
        R.op("dve", lambda e, src=src: e.scalar_tensor_tensor(out=outv, in0=src[:, :, 15:E], scalar=1.0 / w, in1=ev[:, :, 15:E],
                                                              op0=ALU.mult, op1=ALU.subtract), r=[sk, exk], w=[pooled_k[g]])
        if first_main:
            tt_, tk_ = TMP()
            R.op("dve", lambda e, src=src: e.tensor_tensor(out=tt_[:, 0:16], in0=src[:, 0, 15:31], in1=invc[:, g, :], op=ALU.mult),
                 r=[sk, invc_k], w=[tk_])
            R.op("dve", lambda e: e.tensor_tensor(out=pooledT[:, g, 0:16], in0=tt_[:, 0:16], in1=ev[:, 0, 15:31], op=ALU.subtract),
                 r=[tk_, exk], w=[pooled_k[g]])
        if save_carry:
            R.op("dve", lambda e: e.tensor_copy(out=carry[:, g, :], in_=ev[:, 0, L:L + 15]), r=[exk], w=[carry_k[g]])

    def hgrn_stage1(sub, CS, need_A):
        c0 = sub * 128
        NC2 = 128 // CS
        res = {}
        if need_A:
            a = sub % 2
            mk = mask64 if CS == 64 else mask32
            for hg in range(2):
                pst, pk = PS()
                for hh in range(4):
                    h = hg * 4 + hh
                    R.op("pe", lambda e, h=h, hh=hh, pst=pst: e.matmul(pst[:, hh * 128:(hh + 1) * 128], lhsT=big[:, 8 + h, c0:c0 + 128],
                                                              rhs=big[:, h, c0:c0 + 128], start=True, stop=True),
                         r=[big_k[8 + h], big_k[h]], w=[pk], inc=(hh == 3))
                R.op("dve", lambda e, hg=hg, pst=pst: e.tensor_tensor(
                    out=ATm2[:, a, hg * 4:(hg + 1) * 4, :], in0=pst.rearrange("p (h t) -> p h t", h=4),
                    in1=mk[:].unsqueeze(1).broadcast_to([128, 4, 128]), op=ALU.mult), r=[pk, mask_k], w=[ATm2_k[a][hg]])
        a2 = 0
        pst, pk = PS()
        psb = pst.bitcast(BF16).rearrange("p (b t) -> p b t", b=8)
        for h in range(8):
            R.op("pe", lambda e, h=h: e.transpose(out=psb[:, h, :], in_=big[:, 16 + h, c0:c0 + 128], identity=identb[:]),
                 r=[big_k[16 + h], ident_k], w=[pk], inc=(h == 7))
        R.op("act", lambda e: e.activation(out=khat[:, a2, :, :], in_=psb, func=AF.Copy), r=[pk], w=[khat_k[a2]])
        dS = []
        for c in range(NC2):
            row = []
            for hg in range(2):
                pst, pk = PS()
                for hh in range(4):
                    h = hg * 4 + hh
                    kw = {"tile_position": (96, 0)} if c * CS == 96 else {}
                    R.op("pe", lambda e, h=h, hh=hh, c=c, pst=pst, kw=kw: e.matmul(
                        pst[:, hh * 128:(hh + 1) * 128], lhsT=khat[c * CS:(c + 1) * CS, a2, h, :],
                        rhs=vtm[c * CS:(c + 1) * CS, sub, h * 128:(h + 1) * 128], start=True, stop=True, **kw),
                         r=[khat_k[a2], vtm_k[sub]], w=[pk], inc=(hh == 3))
                row.append((pst, pk))
            dS.append(row)
        res["dS"] = dS
        return res

    ATm2 = sb("ATm2", [128, 2, 8, 128], BF16); ATm2_k = [[Tok("ATm2_%d_%d" % (a, b)) for b in range(2)] for a in range(2)]

    def state_update(cur, h, ech, dS_ps, dS_k, hh, nxt=None):
        dst = cur if nxt is None else nxt
        R.op("dve", lambda e: e.scalar_tensor_tensor(out=Sst[:, dst, h, :], in0=Sst[:, cur, h, :], scalar=ech,
                                                     in1=dS_ps[:, hh * 128:(hh + 1) * 128], op0=ALU.mult, op1=ALU.add),
             r=[Sst_k[cur][h], dS_k, elast_k[h]], w=[Sst_k[dst][h]])

    def state_update_all(cur, ch, banks):
        R.op("dve", lambda e: e.tensor_tensor(out=Sst[:, cur, :, :], in0=Sst[:, cur, :, :],
                                              in1=elast[:, :, ch:ch + 1].to_broadcast([128, 8, 128]), op=ALU.mult),
             r=Sst_k[cur] + elast_k, w=Sst_k[cur])
        for hg in range(2):
            pst, pk = banks[hg]
            R.op("dve", lambda e, hg=hg, pst=pst: e.tensor_tensor(out=Sst[:, cur, hg * 4:(hg + 1) * 4, :],
                                                                  in0=pst.rearrange("p (h v) -> p h v", h=4),
                                                                  in1=Sst[:, cur, hg * 4:(hg + 1) * 4, :], op=ALU.add),
                 r=[pk] + Sst_k[cur][hg * 4:(hg + 1) * 4], w=Sst_k[cur][hg * 4:(hg + 1) * 4])

    def sbv(i):
        if i < NSB:
            return Sb[:, i, :, :], Sb_k[i]
        return Eb[:, i - NSB, :].bitcast(BF16).rearrange("p (h v) -> p h v", h=8), Eb_k[i - NSB]

    def cast_state(cur):
        i = state["sbv"] % (NSB + 4)
        state["sbv"] += 1
        ap, tk = sbv(i)
        R.op("act", lambda e: e.activation(out=ap, in_=Sst[:, cur, :, :], func=AF.Copy), r=Sst_k[cur], w=[tk])
        return i

    def hgrn_A(sub, CS):
        c0 = sub * 128
        a = sub % 2
        mk = mask64 if CS == 64 else mask32
        for hg in range(2):
            pst, pk = PSP("hA")
            for hh in range(4):
                h = hg * 4 + hh
                R.op("pe", lambda e, h=h, hh=hh, pst=pst: e.matmul(pst[:, hh * 128:(hh + 1) * 128], lhsT=big[:, 8 + h, c0:c0 + 128],
                                                                   rhs=big[:, h, c0:c0 + 128], start=True, stop=True),
                     r=[big_k[8 + h], big_k[h]], w=[pk], inc=(hh == 3))
            R.op("dve", lambda e, hg=hg, pst=pst: e.tensor_tensor(
                out=ATm2[:, a, hg * 4:(hg + 1) * 4, :], in0=pst.rearrange("p (h t) -> p h t", h=4),
                in1=mk[:].unsqueeze(1).broadcast_to([128, 4, 128]), op=ALU.mult), r=[pk, mask_k], w=[ATm2_k[a][hg]])

    sq4 = Sst[:, 1, :, :].rearrange("p h v -> p (h v)").bitcast(BF16).rearrange("p (s t) -> p s t", s=4)
    obank = {}

    def hgrn_oA(sub, CS, sbv_list):
        c0 = sub * 128
        a = sub % 2
        NC2 = 128 // CS
        for hg in range(2):
            pst, pk = PSP("hO")
            for hh in range(4):
                h = hg * 4 + hh
                o_ps = pst[:, hh * 128:(hh + 1) * 128]
                R.op("pe", lambda e, h=h, o_ps=o_ps: e.matmul(o_ps, lhsT=vtm[:, sub, h * 128:(h + 1) * 128], rhs=ATm2[:, a, h, :],
                                                              start=True, stop=False, skip_group_check=True),
                     r=[vtm_k[sub], ATm2_k[a][hg]], w=[pk], inc=False)
                for c in range(NC2):
                    sap, stk = sbv(sbv_list[c])
                    R.op("pe", lambda e, h=h, c=c, sap=sap, o_ps=o_ps: e.matmul(
                        o_ps[:, c * CS:(c + 1) * CS], lhsT=sap[:, h, :], rhs=big[:, h, c0 + c * CS:c0 + (c + 1) * CS],
                        start=False, stop=(c == NC2 - 1), skip_group_check=True),
                         r=[stk, big_k[h]], w=[pk], inc=(hh == 3 and c == NC2 - 1))
            si = (sub % 2) * 2 + hg
            sqk = [Sst_k[1][2 * si], Sst_k[1][2 * si + 1]]
            R.op("act", lambda e, pst=pst, si=si: e.activation(out=sq4[:, si, :], in_=pst, func=AF.Square), r=[pk], w=sqk)
            obank[(sub, hg)] = (pst, pk, si, sqk)

    def hgrn_oB(sub):
        c0 = sub * 128
        for hg in range(2):
            pst, pk, si, sqk = obank.pop((sub, hg))
            ps_reserved.discard(ps_k.index(pk))
            ms, mk_ = PSP("hA")
            R.op("pe", lambda e, ms=ms, si=si: e.matmul(ms, lhsT=onesb[:], rhs=sq4[:, si, :], start=True, stop=True),
                 r=sqk + [ident_k], w=[mk_])
            t0, k0 = TMP()
            R.op("act", lambda e, ms=ms, t0=t0: e.activation(out=t0, in_=ms, func=AF.Ln, bias=cst[:, 2:3], scale=1.0),
                 r=[mk_, cst_k], w=[k0])
            R.op("act", lambda e, t0=t0: e.activation(out=t0, in_=t0, func=AF.Exp, scale=-0.5), r=[k0], w=[k0])
            R.op("dve", lambda e, pst=pst, t0=t0: e.tensor_tensor(out=t0, in0=pst, in1=t0, op=ALU.mult), r=[pk, k0], w=[k0])
            R.op("dve", lambda e, hg=hg, t0=t0: e.tensor_tensor(
                out=oT[:, hg * 4:(hg + 1) * 4, c0:c0 + 128], in0=t0.rearrange("p (h t) -> p h t", h=4),
                in1=sgT[:, hg * 4:(hg + 1) * 4, c0:c0 + 128], op=ALU.mult),
                 r=[k0] + sgT_k[hg * 4:(hg + 1) * 4], w=[oT_k[sub]])

    def hgrn_out(sub, CS, sbv_list):
        c0 = sub * 128
        a = sub % 2
        NC2 = 128 // CS
        for hg in range(2):
            pst, pk = PS()
            for hh in range(4):
                h = hg * 4 + hh
                o_ps = pst[:, hh * 128:(hh + 1) * 128]
                R.op("pe", lambda e, h=h, o_ps=o_ps: e.matmul(o_ps, lhsT=vtm[:, sub, h * 128:(h + 1) * 128], rhs=ATm2[:, a, h, :],
                                                              start=True, stop=False, skip_group_check=True),
                     r=[vtm_k[sub], ATm2_k[a][hg]], w=[pk], inc=False)
                for c in range(NC2):
                    sap, stk = sbv(sbv_list[c])
                    R.op("pe", lambda e, h=h, c=c, sap=sap, o_ps=o_ps: e.matmul(
                        o_ps[:, c * CS:(c + 1) * CS], lhsT=sap[:, h, :], rhs=big[:, h, c0 + c * CS:c0 + (c + 1) * CS],
                        start=False, stop=(c == NC2 - 1), skip_group_check=True),
                         r=[stk, big_k[h]], w=[pk], inc=(hh == 3 and c == NC2 - 1))
            sa = 0
            R.op("act", lambda e, pst=pst, sa=sa: e.activation(out=sq[:, sa, :], in_=pst, func=AF.Square), r=[pk], w=[sq_k[sa]])
            ms, mk_ = PS()
            R.op("pe", lambda e, ms=ms, sa=sa: e.matmul(ms, lhsT=onesb[:], rhs=sq[:, sa, :], start=True, stop=True),
                 r=[sq_k[sa], ident_k], w=[mk_])
            t0, k0 = TMP()
            R.op("act", lambda e, ms=ms, t0=t0: e.activation(out=t0, in_=ms, func=AF.Ln, bias=cst[:, 2:3], scale=1.0),
                 r=[mk_, cst_k], w=[k0])
            R.op("act", lambda e, t0=t0: e.activation(out=t0, in_=t0, func=AF.Exp, scale=-0.5), r=[k0], w=[k0])
            R.op("dve", lambda e, pst=pst, t0=t0: e.tensor_tensor(out=t0, in0=pst, in1=t0, op=ALU.mult), r=[pk, k0], w=[k0])
            R.op("dve", lambda e, hg=hg, t0=t0: e.tensor_tensor(
                out=oT[:, hg * 4:(hg + 1) * 4, c0:c0 + 128], in0=t0.rearrange("p (h t) -> p h t", h=4),
                in1=sgT[:, hg * 4:(hg + 1) * 4, c0:c0 + 128], op=ALU.mult),
                 r=[k0] + sgT_k[hg * 4:(hg + 1) * 4], w=[oT_k[sub]])

    def tile_load(xsrc, psrc, row0, NS, xs):
        for sub in range(NS):
            R.dma("sp", xt[:, xs + sub, :], xsrc[row0 + sub * 128:row0 + (sub + 1) * 128, :], sem="x%d" % (xs + sub), w=[xt_k[xs + sub]])
        if psrc is not None:
            for sub in range(NS):
                R.dma("sp", pt[:, sub, :], psrc[row0 + sub * 128:row0 + (sub + 1) * 128, :], sem="p%d" % sub, w=[pt_k[sub]])

    def front_A(NS, xs, eng="pool"):
        load_ln("in")
        for sub in range(NS):
            ln_stats(xs + sub)
        for sub in range(NS):
            ln_affine(xs + sub, eng)

    def front_B(NS, xs, with_p):
        for sub in range(NS):
            ln_cast(xs + sub, sub % 2)
            transpose_to_xT(sub, sub % 2)
            if with_p:
                R.op("pool", lambda e, sub=sub: e.tensor_copy(out=pb[:], in_=pt[:, sub, :]), r=[pt_k[sub]], w=[pb_k])
                pst, pk = PS()
                psb = pst.bitcast(BF16).rearrange("p (b t) -> p b t", b=8)
                for blk in range(2):
                    R.op("pe", lambda e, blk=blk, psb=psb: e.transpose(out=psb[:, blk, :], in_=pb[:, blk * 128:(blk + 1) * 128],
                                                                       identity=identb[:]), r=[pb_k, ident_k], w=[pk], inc=(blk == 1))
                R.op("act", lambda e, sub=sub, psb=psb: e.activation(out=pT[:, :, sub * 128:(sub + 1) * 128], in_=psb[:, 0:2, :],
                                                                     func=AF.Copy), r=[pk], w=[pT_k[sub]])

    def proj_f(name, half, ntok, CS, NS, main):
        slab = next_slab(name)
        prev = None
        for j in range(4):
            ps_ap, pk = fm_proj(slab, j * 128, xT, xT_k[0:NS], ntok)
            ctx = fblock_a(half * 4 + j, ps_ap, pk, ntok, CS, main)
            if prev is not None:
                fblock_b(prev)
            prev = ctx
        fblock_b(prev)

    def proj_q(name, half, ntok, NS):
        slab = next_slab(name)
        for j in range(4):
            ps_ap, pk = fm_proj(slab, j * 128, xT, xT_k[0:NS], ntok)
            qblock(half * 4 + j, ps_ap, pk, ntok)

    def proj_i(name, half, NS):
        sl, slk = next_slab(name)
        for sub in range(NS):
            pst, pk = PS()
            for k in range(8):
                R.op("pe", lambda e, k=k, sub=sub, pst=pst: e.matmul(pst, lhsT=xT[:, k, sub * 128:(sub + 1) * 128],
                                                                     rhs=sl[:, k * 512:(k + 1) * 512], start=(k == 0), stop=(k == 7)),
                     r=[slk, xT_k[sub]], w=[pk], inc=(k == 7))
            R.op("act", lambda e, sub=sub, pst=pst: e.activation(out=vtm[:, sub, half * 512:(half + 1) * 512], in_=pst, func=AF.Copy),
                 r=[pk], w=[vtm_k[sub]])

    def proj_vpool(ntok, nseq, NS, sample, first_main, save_carry, pool_dst):
        slab = next_slab("v")
        sl, slk = slab
        for g in range(4):
            ps_ap, pk = fm_proj(slab, g * 128, xT, xT_k[0:NS], ntok)
            pool_group(g, ps_ap, pk, ntok, nseq, sample, first_main, save_carry)
        if pool_dst is not None:
            L = ntok // nseq
            for j in range(nseq):
                pst, pk = PS()
                t0_ = j * L + L - 15
                for k in range(8):
                    R.op("pe", lambda e, k=k, pst=pst, t0_=t0_: e.matmul(pst[0:15, :], lhsT=xT[:, k, t0_:t0_ + 15],
                                                                         rhs=sl[:, k * 512:(k + 1) * 512], start=(k == 0), stop=(k == 7)),
                         r=[slk] + xT_k[0:NS], w=[pk], inc=(k == 7))
                tq, tqk = TMP()
                R.op("act", lambda e, tq=tq, pst=pst: e.activation(out=tq[0:15, :], in_=pst[0:15, :], func=AF.Copy),
                     r=[pk], w=[tqk])
                R.dma("act", pool_dst[j], tq[0:15, :], sem="po", r=[tqk])

    def pm_proj(ntok):
        for g in range(4):
            pst, pk = PS()
            R.op("pe", lambda e, g=g, pst=pst: e.matmul(pst[:, 0:ntok], lhsT=wpm[:, g, :], rhs=pooledT[:, g, 0:ntok], start=True, stop=True),
                 r=[wpm_k, pooled_k[g]], w=[pk])
            R.op("dve", lambda e, g=g, pst=pst: e.tensor_scalar(out=pmT[:, g, 0:ntok], in0=pst[:, 0:ntok], scalar1=pscT[:, g:g + 1],
                                                                scalar2=None, op0=ALU.mult), r=[pk, vec_k], w=[pmT_k[g]])

    def proj_g(name, half, ntok, NS):
        slab = next_slab(name)
        for j in range(4):
            blk = half * 4 + j
            ps_ap, pk = fm_proj(slab, j * 128, xT, xT_k[0:NS], ntok)
            t0, k0 = TMP()
            R.op("act", lambda e, t0=t0, ps_ap=ps_ap: e.activation(out=t0[:, 0:ntok], in_=ps_ap, func=AF.Silu), r=[pk], w=[k0])
            R.op("dve", lambda e, t0=t0, blk=blk: e.tensor_scalar(out=sgT[:, blk, 0:ntok], in0=t0[:, 0:ntok], scalar1=hgT[:, blk:blk + 1],
                                                                   scalar2=None, op0=ALU.mult), r=[k0, vec_k], w=[sgT_k[blk]])

    def proj_gate(nm, ntok, NS):
        dst, dk = (gaS, gaS_k) if nm.startswith("ga") else (gbS, gbS_k)
        slab = next_slab(nm)
        for j in range(4):
            ps_ap, pk = fm_proj(slab, j * 128, xT, xT_k[0:NS], ntok)
            R.op("act", lambda e, dst=dst, j=j, ps_ap=ps_ap: e.activation(out=dst[:, j, 0:ntok], in_=ps_ap, func=AF.Sigmoid),
                 r=[pk], w=[dk[j]])

    def merge_phase(ntok, NS, xs, gates0_done=False):
        for h in range(2):
            if not (h == 0 and gates0_done):
                proj_gate("ga%d" % h, ntok, NS)
                proj_gate("gb%d" % h, ntok, NS)
            slab_b = next_slab("wb%d" % h)
            slab_a = next_slab("wa%d" % h)
            for j in range(4):
                yb, ybk = fm_proj(slab_b, j * 128, pmT, pmT_k, ntok, nk=4)
                ya, yak = fm_proj(slab_a, j * 128, oT, oT_k[0:NS], ntok)
                t2, k2 = TMP(); t1, k1 = TMP()
                R.op("dve", lambda e, t2=t2, yb=yb, j=j: e.tensor_tensor(out=t2[:, 0:ntok], in0=yb, in1=gbS[:, j, 0:ntok], op=ALU.mult),
                     r=[ybk, gbS_k[j]], w=[k2])
                R.op("dve", lambda e, t1=t1, ya=ya, j=j: e.tensor_tensor(out=t1[:, 0:ntok], in0=ya, in1=gaS[:, j, 0:ntok], op=ALU.mult),
                     r=[yak, gaS_k[j]], w=[k1])
                R.op("pool", lambda e, t1=t1, t2=t2, h=h, j=j: e.tensor_tensor(out=big[:, h * 4 + j, 0:ntok], in0=t1[:, 0:ntok],
                                                                               in1=t2[:, 0:ntok], op=ALU.add),
                     r=[k1, k2], w=[big_k[h * 4 + j]])
        sl0, slk0 = next_slab("wo0")
        sl1, slk1 = next_slab("wo1")
        load_ln("1")
        for sub in range(NS):
            for h2, (sl, slk) in enumerate(((sl0, slk0), (sl1, slk1))):
                pst, pk = PS()
                for k in range(8):
                    R.op("pe", lambda e, k=k, sub=sub, pst=pst, sl=sl: e.matmul(pst, lhsT=big[:, k, sub * 128:(sub + 1) * 128],
                                                                                rhs=sl[:, k * 512:(k + 1) * 512], start=(k == 0), stop=(k == 7)),
                         r=[slk, big_k[k]], w=[pk], inc=(k == 7))
                R.op("dve", lambda e, sub=sub, pst=pst, h2=h2: e.scalar_tensor_tensor(
                    out=xt[:, xs + sub, h2 * 512:(h2 + 1) * 512], in0=xt[:, xs + sub, h2 * 512:(h2 + 1) * 512], scalar=ALPHA, in1=pst,
                    op0=ALU.mult, op1=ALU.add), r=[pk, xt_k[xs + sub]], w=[xt_k[xs + sub]])
            ln_stats(xs + sub)
        for sub in range(NS):
            ln_affine(xs + sub)
        for sub in range(NS):
            ln_cast(xs + sub, sub % 2)
            transpose_to_xT(sub, sub % 2)

    def ffn_phase(ntok, NS, ydst, row0, xs, mid_hook=None):
        for s in range(11):
            sl, slk = next_slab("up%d" % s)
            for jj in range(2):
                j = 2 * s + jj
                gps, gk = PS()
                ups, uk = PS()
                for k in range(8):
                    R.op("pe", lambda e, k=k, jj=jj, gps=gps, sl=sl: e.matmul(gps[:, 0:ntok], lhsT=sl[:, k * 256 + jj * 128:k * 256 + jj * 128 + 128],
                                                                       rhs=xT[:, k, 0:ntok], start=(k == 0), stop=(k == 7)),
                         r=[slk] + xT_k[0:NS], w=[gk], inc=(k == 7))
                for k in range(8):
                    R.op("pe", lambda e, k=k, jj=jj, ups=ups, sl=sl: e.matmul(ups[:, 0:ntok], lhsT=sl[:, 2048 + k * 256 + jj * 128:2048 + k * 256 + jj * 128 + 128],
                                                                       rhs=xT[:, k, 0:ntok], start=(k == 0), stop=(k == 7)),
                         r=[slk] + xT_k[0:NS], w=[uk], inc=(k == 7))
                t0, k0 = TMP()
                R.op("act", lambda e, t0=t0, gps=gps: e.activation(out=t0[:, 0:ntok], in_=gps[:, 0:ntok], func=AF.Silu), r=[gk], w=[k0])
                R.op("dve", lambda e, t0=t0, ups=ups, j=j: e.tensor_tensor(out=big[:, j, 0:ntok], in0=t0[:, 0:ntok], in1=ups[:, 0:ntok], op=ALU.mult),
                     r=[k0, uk], w=[big_k[j]])
        load_ln("2")
        for h in range(2):
            sl, slk = next_slab("pg%d" % h)
            slp, slpk = next_slab("pp%d" % h)
            for sub in range(NS):
                gps, gk = PS()
                pps, pk = PS()
                for k in range(8):
                    R.op("pe", lambda e, k=k, sub=sub, gps=gps, sl=sl: e.matmul(gps, lhsT=xT[:, k, sub * 128:(sub + 1) * 128],
                                                                                rhs=sl[:, k * 512:(k + 1) * 512], start=(k == 0), stop=(k == 7)),
                         r=[slk, xT_k[sub]], w=[gk], inc=(k == 7))
                for k in range(2):
                    R.op("pe", lambda e, k=k, sub=sub, pps=pps, slp=slp: e.matmul(pps, lhsT=pT[:, k, sub * 128:(sub + 1) * 128],
                                                                                  rhs=slp[:, k * 512:(k + 1) * 512], start=(k == 0), stop=(k == 1)),
                         r=[slpk, pT_k[sub]], w=[pk], inc=(k == 1))
                t0, k0 = TMP()
                R.op("act", lambda e, t0=t0, gps=gps: e.activation(out=t0, in_=gps, func=AF.Sigmoid), r=[gk], w=[k0])
                R.op("dve", lambda e, t0=t0, pps=pps: e.tensor_tensor(out=t0, in0=t0, in1=pps, op=ALU.mult), r=[k0, pk], w=[k0])
                R.op("dve", lambda e, t0=t0, sub=sub, h=h: e.scalar_tensor_tensor(
                    out=xt[:, xs + sub, h * 512:(h + 1) * 512], in0=xt[:, xs + sub, h * 512:(h + 1) * 512], scalar=ALPHA, in1=t0,
                    op0=ALU.mult, op1=ALU.add), r=[k0, xt_k[xs + sub]], w=[xt_k[xs + sub]])
            if h == 1 and mid_hook is not None:
                mid_hook()
            accs = [PS() for _ in range(NS)]
            for kg in range(3):
                nk = 8 if kg < 2 else 6
                sl, slk = next_slab("dn%d%d" % (h, kg))
                for sub in range(NS):
                    aps, ak = accs[sub]
                    for kk in range(nk):
                        k = kg * 8 + kk
                        R.op("pe", lambda e, kk=kk, k=k, sub=sub, aps=aps, sl=sl: e.matmul(aps, lhsT=big[:, k, sub * 128:(sub + 1) * 128],
                                                                                           rhs=sl[:, kk * 512:(kk + 1) * 512], start=(k == 0), stop=(k == 21)),
                             r=[slk, big_k[k]], w=[ak], inc=(kk == nk - 1))
            for sub in range(NS):
                aps, ak = accs[sub]
                R.op("dve", lambda e, sub=sub, aps=aps, h=h: e.tensor_tensor(out=xt[:, xs + sub, h * 512:(h + 1) * 512], in0=aps,
                                                                              in1=xt[:, xs + sub, h * 512:(h + 1) * 512], op=ALU.add),
                     r=[ak, xt_k[xs + sub]], w=[xt_k[xs + sub]])

        def ln2_tail():
            assert state["ln"] == "2"
            for sub in range(NS):
                ln_stats(xs + sub)
            for sub in range(NS):
                ln_affine(xs + sub)
                R.dma("pool", ydst[row0 + sub * 128:row0 + (sub + 1) * 128, :], xt[:, xs + sub, :], sem="y%d" % (xs + sub),
                      r=[xt_k[xs + sub]])
        return ln2_tail

    tiles = []
    for t in range(n_pre):
        tiles.append(dict(kind="pre", xsrc=x_pre, psrc=None, row0=t * TT, NS=4, last=(t == n_pre - 1)))
    for t in range(n_main):
        tiles.append(dict(kind="main", xsrc=x_main, psrc=p_main, row0=t * TT, NS=4, ydst=y_main, ntok=TT, CS=64, nseq=1,
                          first=(t == 0), last=(t == n_main - 1)))
    if do_sample:
        tiles.append(dict(kind="sample", xsrc=x_s, psrc=p_s, row0=0, NS=1, ydst=y_s, ntok=128, CS=32, nseq=4, first=False, last=False))
    for i, T in enumerate(tiles):
        T["xs"] = (i % 2) * 4

    def prefetch_load(i):
        if i < len(tiles):
            T = tiles[i]
            tile_load(T["xsrc"], T["psrc"], T["row0"], T["NS"], T["xs"])

    def prefetch_front(i, eng="pool"):
        if i < len(tiles):
            T = tiles[i]
            front_A(T["NS"], T["xs"], eng)

    def pre_tile(i, T):
        xs = T["xs"]
        if not T.get("fb_done"):
            front_B(4, xs, False)
        prefetch_load(i + 1)
        proj_f("f0", 0, TT, 64, 4, False)
        proj_f("f1", 1, TT, 64, 4, False)
        prefetch_front(i + 1, "dve")
        proj_i("i0", 0, 4)
        proj_i("i1", 1, 4)
        if T["last"]:
            slab = next_slab("v")
            for g in range(4):
                ps_ap, pk = fm_proj(slab, g * 128, xT, xT_k[0:4], TT)
                R.op("act", lambda e, g=g, ps_ap=ps_ap: e.activation(out=carry[:, g, :], in_=ps_ap[:, TT - 15:TT], func=AF.Copy),
                     r=[pk], w=[carry_k[g]])
        if i + 1 < len(tiles) and tiles[i + 1]["kind"] == "pre":
            front_B(4, tiles[i + 1]["xs"], False)
            tiles[i + 1]["fb_done"] = True
        for sub in range(4):
            res = hgrn_stage1(sub, 64, False)
            for c in range(2):
                state_update_all(0, sub * 2 + c, res["dS"][c])
        cast_some(5)
        if T["last"]:
            for h in range(8):
                R.op("dve", lambda e, h=h: e.tensor_scalar(out=Sst[:, 0, h, :], in0=Sst[:, 0, h, :], scalar1=metat[:, 0:1], scalar2=None,
                                                           op0=ALU.mult), r=[Sst_k[0][h], meta_k], w=[Sst_k[0][h]])
            for g in range(4):
                R.op("dve", lambda e, g=g: e.tensor_scalar(out=carry[:, g, :], in0=carry[:, g, :], scalar1=metat[:, 0:1], scalar2=None,
                                                           op0=ALU.mult), r=[carry_k[g], meta_k], w=[carry_k[g]])
            cast_some(NSLAB)

    def main_tile(i, T):
        xs = T["xs"]; ntok = T["ntok"]; CS = T["CS"]; nseq = T["nseq"]; NS = T["NS"]
        sample = (T["kind"] == "sample")
        NC2 = 128 // CS
        if not T.get("fb_done"):
            front_B(NS, xs, True)
        proj_f("f0", 0, ntok, CS, NS, True)
        proj_q("q0", 0, ntok, NS)
        proj_f("f1", 1, ntok, CS, NS, True)
        proj_q("q1", 1, ntok, NS)
        proj_i("i0", 0, NS)
        proj_i("i1", 1, NS)
        if state.get("pending_ln2") is not None:
            state["pending_ln2"]()
            state["pending_ln2"] = None
        prefetch_load(i + 1)
        pool_dst = None
        if sample:
            pool_dst = [pools_out[j] for j in range(4)]
        elif T["last"]:
            pool_dst = [pool_out]
        if not sample:
            sbi = state.get("sb_last")
            if sbi is None:
                sbi = cast_state(0)
            sbls = []

            def chain_sub(sub, sbi):
                res = hgrn_stage1(sub, CS, False)
                sbl = [sbi]
                for c in range(NC2):
                    state_update_all(0, sub * NC2 + c, res["dS"][c])
                    sbi = cast_state(0)
                    if c < NC2 - 1:
                        sbl.append(sbi)
                sbls.append(sbl)
                return sbi

            sbi = chain_sub(0, sbi)
            proj_g("g0", 0, ntok, NS)
            sbi = chain_sub(1, sbi)
            proj_g("g1", 1, ntok, NS)
            sbi = chain_sub(2, sbi)
            hgrn_A(0, CS)
            hgrn_A(1, CS)
            proj_vpool(ntok, nseq, NS, sample, T["first"], True, pool_dst)
            hgrn_oA(0, CS, sbls[0])
            sbi = chain_sub(3, sbi)
            state["sb_last"] = sbi
            if T["last"]:
                R.dma("act", S_out.rearrange("h k v -> k h v"), Sst[:, 0, :, :], sem="sout", r=Sst_k[0])
            hgrn_A(2, CS)
            hgrn_oA(1, CS, sbls[1])
            proj_gate("ga0", ntok, NS)
            hgrn_oB(0)
            hgrn_A(3, CS)
            hgrn_oA(2, CS, sbls[2])
            proj_gate("gb0", ntok, NS)
            hgrn_oB(1)
            hgrn_oA(3, CS, sbls[3])
            hgrn_oB(2)
            hgrn_oB(3)
            pm_proj(ntok)
        else:
            proj_g("g0", 0, ntok, NS)
            proj_g("g1", 1, ntok, NS)
            proj_vpool(ntok, nseq, NS, sample, T["first"], False, pool_dst)
            pm_proj(ntok)
            res = hgrn_stage1(0, CS, True)
            sbl = []
            for j in range(4):
                a = j % 2
                R.dma("pool", Sst[:, a, :, :], s_hgrn[j].rearrange("h k v -> k h v"), sem="sl%d" % a, w=Sst_k[a])
                sbl.append(cast_state(a))
                state_update_all(a, j, res["dS"][j])
                R.dma("act", Ss_out[j].rearrange("h k v -> k h v"), Sst[:, a, :, :], sem="so%d" % a, r=Sst_k[a])
            hgrn_out(0, CS, sbl)
        merge_phase(ntok, NS, xs, gates0_done=(not sample))
        prefetch_front(i + 1)
        def hook():
            if i + 1 < len(tiles):
                Tn = tiles[i + 1]
                front_B(Tn["NS"], Tn["xs"], True)
                Tn["fb_done"] = True
        tail = ffn_phase(ntok, NS, T["ydst"], T["row0"], xs, mid_hook=hook)
        if i + 1 < len(tiles):
            state["pending_ln2"] = tail
        else:
            tail()

    prefetch_load(0)
    prefetch_front(0, "dve" if n_pre > 0 else "pool")
    for i, T in enumerate(tiles):
        if T["kind"] == "pre":
            pre_tile(i, T)
        else:
            main_tile(i, T)

    for key in sorted(R.isdma):
        if key.startswith("y") or key.startswith("so") or key == "po":
            R._wait("act", key, R.cnt[key])

    assert slab_pos["i"] == len(slab_seq), (slab_pos["i"], len(slab_seq))
    with nc.Block() as block:
        @block.tensor
        def _(e):
            for f in R.ops["pe"]:
                f(e)

        @block.scalar
        def _(e):
            for f in R.ops["act"]:
                f(e)

        @block.vector
        def _(e):
            for f in R.ops["dve"]:
                f(e)

        @block.gpsimd
        def _(e):
            for f in R.ops["pool"]:
                f(e)

        @block.sync
        def _(e):
            for f in R.ops["sp"]:
                f(e)
    stack.close()
    return nc, {e: len(R.ops[e]) for e in R.ENG}


_CACHE = {}


def make_in_maps(inp, n_pre=8, n_main=8):
    f = lambda a: np.ascontiguousarray(np.asarray(a, dtype=np.float32))
    x_prompt = f(inp["x_prompt"]); x_sample = f(inp["x_sample"]); p_prompt = f(inp["p_prompt"]); p_sample = f(inp["p_sample"])
    state_hgrn = f(inp["state_hgrn"]); state_pool = f(inp["state_pool"])
    seg = n_main * TT
    npre = max(n_pre, 1) * TT
    shared = {
        "ln_in_g": f(inp["ln_in_g"]), "ln_in_b": f(inp["ln_in_b"]), "lb_logits": f(inp["lb_logits"]), "w_in": f(inp["w_in"])[0],
        "hg_g": f(inp["hgrn_norm_g"])[0], "w_a": f(inp["w_branch_a"])[0], "w_pm": f(inp["w_pool_mix"])[0],
        "p_scale": f(inp["pool_scale"])[0], "w_b": f(inp["w_branch_b"])[0], "w_o": f(inp["w_out"])[0],
        "ln1_g": f(inp["ln1_g"])[0], "ln1_b": f(inp["ln1_b"])[0], "w_up": f(inp["w_ffn_up"])[0],
        "w_down": f(inp["w_ffn_down"])[0], "w_pp": f(inp["w_ple_proj"])[0], "w_pg": f(inp["w_ple_gate"])[0],
        "ln2_g": f(inp["ln2_g"])[0], "ln2_b": f(inp["ln2_b"])[0],
    }
    in_maps = []
    for c in range(NCORE):
        j, half = c // 2, c % 2
        m = dict(shared)
        m["x_main"] = np.ascontiguousarray(x_prompt[j, half * seg:(half + 1) * seg])
        m["x_pre"] = np.ascontiguousarray(x_prompt[j, 0:npre])
        m["p_main"] = np.ascontiguousarray(p_prompt[0, j, half * seg:(half + 1) * seg])
        m["x_s"] = np.ascontiguousarray(x_sample[4 * c:4 * c + 4].reshape(128, D))
        m["p_s"] = np.ascontiguousarray(p_sample[0, 4 * c:4 * c + 4].reshape(128, 256))
        m["s_hgrn"] = np.ascontiguousarray(state_hgrn[0, 4 * c:4 * c + 4])
        m["s_pool"] = np.ascontiguousarray(state_pool[0, 4 * c:4 * c + 4])
        meta = np.zeros((128, 2), np.float32)
        meta[:, 0] = float(half)
        meta[:, 1] = float(half * seg)
        m["meta"] = meta
        in_maps.append(m)
    return in_maps


def gather(rs, n_main=8):
    seg = n_main * TT
    y_prompt = np.zeros((4, 2 * seg, D), np.float32)
    y_sample = np.zeros((32, 32, D), np.float32)
    hs_p = np.zeros((1, 4, 8, 128, 128), np.float32)
    pl_p = np.zeros((1, 4, 15, 512), np.float32)
    hs_s = np.zeros((1, 32, 8, 128, 128), np.float32)
    pl_s = np.zeros((1, 32, 15, 512), np.float32)
    for c in range(NCORE):
        j, half = c // 2, c % 2
        y_prompt[j, half * seg:(half + 1) * seg] = rs[c]["y_main"]
        y_sample[4 * c:4 * c + 4] = rs[c]["y_s"].reshape(4, 32, D)
        hs_s[0, 4 * c:4 * c + 4] = rs[c]["Ss_out"]
        pl_s[0, 4 * c:4 * c + 4] = rs[c]["pools_out"]
        if half == 1:
            hs_p[0, j] = rs[c]["S_out"]
            pl_p[0, j] = rs[c]["pool_out"]
    return (y_prompt, y_sample, hs_p, pl_p, hs_s, pl_s)


def kernel(**inputs):
    if "nc" not in _CACHE:
        _CACHE["nc"] = build_program()[0]
    nc = _CACHE["nc"]
    in_maps = make_in_maps(inputs)
    res = run_bass_kernel_spmd(nc, in_maps, core_ids=list(range(NCORE)))
    return gather(res.results)
```

```python
import os
from contextlib import ExitStack
import numpy as np
import concourse.bass as bass
import concourse.mybir as mybir
from concourse.bass_utils import run_bass_kernel_spmd

F32 = mybir.dt.float32
BF16 = mybir.dt.bfloat16
I32 = mybir.dt.int32
AF = mybir.ActivationFunctionType
ALU = mybir.AluOpType

D = 1024
NCORE = 8
SEQ = 8192
HALF = 4096
TT = 512
DFF = 2816
NBF = 22
ALPHA = float(2.0 ** 0.25)
LN_EPS = 1e-5
RMS_EPS = 1e-6
NSLOT = 4
SLAB = 4096


class Tok:
    __slots__ = ("name", "w", "r")

    def __init__(self, name):
        self.name = name
        self.w = None
        self.r = {}


class Sched:
    ENG = ("pe", "act", "dve", "pool", "sp")

    def __init__(self, nc, stack):
        self.nc = nc
        self.stack = stack
        self.ops = {e: [] for e in self.ENG}
        self.semh = {}
        self.cnt = {}
        for e in self.ENG:
            self.semh[e] = stack.enter_context(nc.semaphore("s_" + e))
            self.cnt[e] = 0
        self.seen = {e: {} for e in self.ENG}
        self.pending = {e: [] for e in self.ENG}
        self.isdma = set()
        self.nwait = 0

    def dsem(self, name):
        if name not in self.semh:
            self.semh[name] = self.stack.enter_context(self.nc.semaphore("d_" + name))
            self.cnt[name] = 0
            self.isdma.add(name)
        return name

    def _wait(self, eng, key, val):
        if key.startswith("PEND:"):
            if key == "PEND:" + eng and eng == "pe":
                return
            raise RuntimeError("dependency on pending event %s from %s" % (key, eng))
        if key == "pe" and eng == "pe":
            return
        if key in self.isdma:
            val = max(val, self.cnt[key])
        if self.seen[eng].get(key, 0) >= val:
            return
        self.seen[eng][key] = val
        sem = self.semh[key]
        self.nwait += 1
        self.ops[eng].append(lambda e, sem=sem, val=val: e.wait_ge(sem, val))

    def _deps(self, eng, reads, writes):
        for t in reads:
            if t.w is not None:
                self._wait(eng, *t.w)
        for t in writes:
            if t.w is not None:
                self._wait(eng, *t.w)
            for k, v in list(t.r.items()):
                self._wait(eng, k, v)

    def op(self, eng, fn, r=(), w=(), inc=True):
        self._deps(eng, r, w)
        pend = self.pending[eng]
        for t in r:
            pend.append((t, 0))
        for t in w:
            pend.append((t, 1))
            if not inc:
                t.w = ("PEND:" + eng, 0)
                t.r = {}
        if not inc:
            for t in r:
                t.r["PEND:" + eng] = 0
            self.ops[eng].append(lambda e, fn=fn: fn(e))
            return
        self.cnt[eng] += 1
        v = self.cnt[eng]
        sem = self.semh[eng]
        self.ops[eng].append(lambda e, fn=fn, sem=sem: fn(e).then_inc(sem, 1))
        for t, isw in pend:
            if isw:
                t.w = (eng, v)
                t.r = {}
            else:
                t.r.pop("PEND:" + eng, None)
                t.r[eng] = v
        pend.clear()

    def dma(self, queue, out, in_, sem, r=(), w=(), **kw):
        self._deps(queue, r, w)
        self.dsem(sem)
        self.cnt[sem] += 16
        v = self.cnt[sem]
        h = self.semh[sem]
        self.ops[queue].append(
            lambda e, out=out, in_=in_, h=h, kw=kw: e.dma_start(out=out, in_=in_, **kw).then_inc(h, 16))
        for t in r:
            t.r[sem] = v
        for t in w:
            t.w = (sem, v)
            t.r = {}


def build_program(n_pre=8, n_main=8, do_sample=True):
    nc = bass.Bass("TRN2", target_bir_lowering=False)
    stack = ExitStack()

    def din(name, shape):
        return nc.dram_tensor(name, shape, F32, kind="ExternalInput").ap()

    def dout(name, shape):
        return nc.dram_tensor(name, shape, F32, kind="ExternalOutput").ap()

    RM = n_main * TT
    RP = max(n_pre, 1) * TT
    x_main = din("x_main", [RM, D])
    x_pre = din("x_pre", [RP, D])
    p_main = din("p_main", [RM, 256])
    x_s = din("x_s", [128, D])
    p_s = din("p_s", [128, 256])
    s_hgrn = din("s_hgrn", [4, 8, 128, 128])
    s_pool = din("s_pool", [4, 15, 512])
    meta = din("meta", [128, 2])
    ln_in_g = din("ln_in_g", [D]); ln_in_b = din("ln_in_b", [D])
    lb_logits = din("lb_logits", [2, D])
    w_in = din("w_in", [D, 6656])
    hg_g = din("hg_g", [D])
    w_a = din("w_a", [D, D])
    w_pm = din("w_pm", [4, 128, 128])
    p_scale = din("p_scale", [512])
    w_b = din("w_b", [512, D])
    w_o = din("w_o", [D, D])
    ln1_g = din("ln1_g", [D]); ln1_b = din("ln1_b", [D])
    w_up = din("w_up", [D, 2 * DFF])
    w_down = din("w_down", [DFF, D])
    w_pp = din("w_pp", [256, D])
    w_pg = din("w_pg", [D, D])
    ln2_g = din("ln2_g", [D]); ln2_b = din("ln2_b", [D])

    y_main = dout("y_main", [RM, D])
    y_s = dout("y_s", [128, D])
    S_out = dout("S_out", [8, 128, 128])
    pool_out = dout("pool_out", [15, 512])
    Ss_out = dout("Ss_out", [4, 8, 128, 128])
    pools_out = dout("pools_out", [4, 15, 512])

    slabs = []

    def add(name, pieces):
        slabs.append((name, pieces))
        return len(slabs) - 1

    def wcol(wap, c0):
        return [(wap, 0, 8, c0, 512, 0)]

    SL = {}
    SL["f0"] = add("f0", wcol(w_in, 1024)); SL["q0"] = add("q0", wcol(w_in, 0))
    SL["f1"] = add("f1", wcol(w_in, 1536)); SL["q1"] = add("q1", wcol(w_in, 512))
    SL["i0"] = add("i0", wcol(w_in, 2048)); SL["i1"] = add("i1", wcol(w_in, 2560))
    SL["g0"] = add("g0", wcol(w_in, 3072)); SL["g1"] = add("g1", wcol(w_in, 3584))
    SL["v"] = add("v", wcol(w_in, 4096))
    for h in range(2):
        SL["ga%d" % h] = add("ga%d" % h, wcol(w_in, 4608 + 512 * h))
        SL["gb%d" % h] = add("gb%d" % h, wcol(w_in, 5632 + 512 * h))
        SL["wb%d" % h] = add("wb%d" % h, [(w_b, 0, 4, 512 * h, 512, 0)])
        SL["wa%d" % h] = add("wa%d" % h, wcol(w_a, 512 * h))
    SL["wo0"] = add("wo0", wcol(w_o, 0)); SL["wo1"] = add("wo1", wcol(w_o, 512))
    for s in range(11):
        SL["up%d" % s] = add("up%d" % s, [(w_up, 0, 8, 256 * s, 256, 0), (w_up, 0, 8, DFF + 256 * s, 256, 2048)])
    for h in range(2):
        SL["pg%d" % h] = add("pg%d" % h, wcol(w_pg, 512 * h))
        SL["pp%d" % h] = add("pp%d" % h, [(w_pp, 0, 2, 512 * h, 512, 0)])
        for kg in range(3):
            nk = 8 if kg < 2 else 6
            SL["dn%d%d" % (h, kg)] = add("dn%d%d" % (h, kg), [(w_down, 8 * kg, nk, 512 * h, 512, 0)])
    NSLAB = len(slabs)
    scratch = nc.dram_tensor("wscr", [NSLAB, 128, SLAB], BF16, kind="Internal").ap()

    pre_seq = ["f0", "f1", "i0", "i1"]
    main_seq = [s[0] for s in slabs]
    slab_seq = []
    for t in range(n_pre):
        slab_seq += pre_seq + (["v"] if t == n_pre - 1 else [])
    for t in range(n_main + (1 if do_sample else 0)):
        slab_seq += main_seq

    R = Sched(nc, stack)

    def sb(name, shape, dt):
        return stack.enter_context(nc.sbuf_tensor(name, shape, dt))

    wring = sb("wring", [128, NSLOT, SLAB], BF16); wring_k = [Tok("wr%d" % i) for i in range(NSLOT)]
    xt = sb("xt", [128, 8, D], F32); xt_k = [Tok("xt%d" % i) for i in range(8)]
    pt = sb("pt", [128, 4, 256], F32); pt_k = [Tok("pt%d" % i) for i in range(4)]
    xb = sb("xb", [128, 2, D], BF16); xb_k = [Tok("xb0"), Tok("xb1")]
    pb = sb("pb", [128, 256], BF16); pb_k = Tok("pb")
    xT = sb("xT", [128, 8, TT], BF16); xT_k = [Tok("xT%d" % i) for i in range(4)]
    pT = sb("pT", [128, 2, TT], BF16); pT_k = [Tok("pT%d" % i) for i in range(4)]
    big = sb("big", [128, 24, TT], BF16); big_k = [Tok("big%d" % i) for i in range(24)]
    Eb = sb("Eb", [128, 4, TT], F32); Eb_k = [Tok("Eb%d" % i) for i in range(4)]
    NTMP = 7
    tmp = sb("tmp", [128, NTMP, 528], F32); tmp_k = [Tok("tmp%d" % i) for i in range(NTMP)]
    vtm = sb("vtm", [128, 4, D], BF16); vtm_k = [Tok("vtm%d" % i) for i in range(4)]
    sgT = sb("sgT", [128, 8, TT], BF16); sgT_k = [Tok("sgT%d" % i) for i in range(8)]
    oT = sb("oT", [128, 8, TT], BF16); oT_k = [Tok("oT%d" % i) for i in range(4)]
    gaS = sb("gaS", [128, 4, TT], BF16); gaS_k = [Tok("gaS%d" % i) for i in range(4)]
    gbS = sb("gbS", [128, 4, TT], BF16); gbS_k = [Tok("gbS%d" % i) for i in range(4)]
    EXTW = 15 + TT
    pooledT = sb("pooledT", [128, 4, TT], BF16); pooled_k = [Tok("pooled%d" % i) for i in range(4)]
    pmT = sb("pmT", [128, 4, TT], BF16); pmT_k = [Tok("pmT%d" % i) for i in range(4)]
    carry = sb("carry", [128, 4, 15], F32); carry_k = [Tok("carry%d" % i) for i in range(4)]
    spre = sb("spre", [128, 4, 4, 15], F32); spre_k = Tok("spre")
    Sst = sb("Sst", [128, 2, 8, 128], F32); Sst_k = [[Tok("S%d_%d" % (a, h)) for h in range(8)] for a in range(2)]
    NSB = 4
    Sb = sb("Sb", [128, NSB, 8, 128], BF16); Sb_k = [Tok("Sb%d" % i) for i in range(NSB)]
    khat = sb("khat", [128, 1, 8, 128], BF16); khat_k = [Tok("khat0"), Tok("khat0")]
    khat_k[1] = khat_k[0]
    sq = sb("sq", [128, 1, TT], BF16); sq_k = [Tok("sq0")] * 2
    elast = sb("elast", [128, 8, 8], F32); elast_k = [Tok("el%d" % i) for i in range(8)]
    lnG = sb("lnG", [128, D], F32); lnG_k = Tok("lnG")
    lnB = sb("lnB", [128, D], F32); lnB_k = Tok("lnB")
    wpm = sb("wpm", [128, 4, 128], BF16); wpm_k = Tok("wpm")
    NST = 4
    st = sb("st", [128, NST, 12], F32); mv = sb("mv", [128, NST, 2], F32); rs = sb("rs", [128, NST, 4], F32)
    st_k = [Tok("st%d" % i) for i in range(NST)]
    cst = sb("cst", [128, 8], F32); cst_k = Tok("cst")
    identb = sb("identb", [128, 128], BF16); identf = sb("identf", [128, 128], F32); onesb = sb("onesb", [128, 128], BF16)
    onesf = sb("onesf", [128, 128], F32)
    ident_k = Tok("ident")
    mask64 = sb("mask64", [128, 128], BF16); mask32 = sb("mask32", [128, 128], BF16); mask_k = Tok("mask")
    rm64 = sb("rm64", [128, TT], BF16); rm32 = sb("rm32", [128, TT], BF16); rm_k = Tok("rm")
    lbt = sb("lbt", [128, 2, 8], F32)
    oml = sb("oml", [128, 8], F32); noml = sb("noml", [128, 8], F32); lb_k = Tok("lb")
    hgT = sb("hgT", [128, 8], F32); pscT = sb("pscT", [128, 4], F32); vec_k = Tok("vec")
    metat = sb("metat", [128, 2], F32); meta_k = Tok("meta")
    invc = sb("invc", [128, 4, 16], F32); invc_k = Tok("invc")
    iot = sb("iot", [128, 16], I32); iof = sb("iof", [128, 16], F32)

    psum = stack.enter_context(nc.psum_tensor("psum", [128, 8, 512], F32))
    ps_k = [Tok("ps%d" % i) for i in range(8)]
    state = {"ps": 0, "tmp": 0, "slab": 0, "st": 0, "sbv": 0, "ln": None}

    ps_reserved = set()

    def PS():
        while True:
            i = state["ps"] % 8
            state["ps"] += 1
            if i not in ps_reserved:
                return psum[:, i, :], ps_k[i]

    pspools = {"hA": ([0, 1, 2, 3], [0]), "hO": ([4, 5, 6, 7], [0])}

    def PSP(name):
        banks, c = pspools[name]
        i = banks[c[0] % len(banks)]
        c[0] += 1
        if name == "hO":
            ps_reserved.add(i)
        return psum[:, i, :], ps_k[i]

    def TMPW():
        i = state["tmp"] % NTMP
        state["tmp"] += 1
        return tmp[:, i, :], tmp_k[i]

    def TMP():
        i = state["tmp"] % NTMP
        state["tmp"] += 1
        return tmp[:, i, 0:TT], tmp_k[i]

    scr_k = [Tok("scr%d" % i) for i in range(NSLAB)]
    cast_done = set()

    def cast_slab(si):
        if si in cast_done:
            return
        cast_done.add(si)
        name, pieces = slabs[si]
        for (src, k0, nk, c0, ncols, off) in pieces:
            R.dma("pool", scratch[si][:, off:off + nk * ncols].rearrange("p (k c) -> p k c", k=nk),
                  src[k0 * 128:(k0 + nk) * 128, c0:c0 + ncols].rearrange("(k p) c -> p k c", p=128),
                  sem="scr%d" % si, w=[scr_k[si]])

    cast_order = [SL[n] for n in (pre_seq + ["v"])] + [i for i in range(NSLAB)]

    def cast_some(n):
        k = 0
        for si in cast_order:
            if k >= n:
                break
            if si not in cast_done:
                cast_slab(si)
                k += 1


    slab_pos = {"i": 0, "issued": 0}

    def issue_loads(upto):
        while slab_pos["issued"] < min(upto, len(slab_seq)):
            j = slab_pos["issued"]
            si = SL[slab_seq[j]]
            cast_slab(si)
            slot = j % NSLOT
            used = max(off + nk * ncols for (_, _, nk, _, ncols, off) in slabs[si][1])
            R.dma("sp", wring[:, slot, 0:used], scratch[si][:, 0:used], sem="wr%d" % slot, r=[scr_k[si]], w=[wring_k[slot]])
            slab_pos["issued"] += 1

    def next_slab(name):
        i = slab_pos["i"]
        assert slab_seq[i] == name, (slab_seq[i], name, i)
        issue_loads(i + 3)
        slab_pos["i"] += 1
        slot = i % NSLOT
        return wring[:, slot, :], wring_k[slot]

    R.op("pool", lambda e: e.memset(cst[:, 0:1], LN_EPS), w=[cst_k])
    R.op("pool", lambda e: e.memset(cst[:, 1:2], 1.0), w=[cst_k])
    R.op("pool", lambda e: e.memset(cst[:, 2:3], RMS_EPS), w=[cst_k])
    R.op("pool", lambda e: e.memset(onesf[:], 1.0), w=[ident_k])
    R.op("pool", lambda e: e.memset(onesb[:], 1.0 / 128.0), w=[ident_k])
    R.op("pool", lambda e: e.affine_select(out=identf[:], in_=onesf[:], pattern=[[-1, 128]], compare_op=ALU.is_equal,
                                           fill=0.0, base=0, channel_multiplier=1), r=[ident_k], w=[ident_k])
    R.op("pool", lambda e: e.tensor_copy(out=identb[:], in_=identf[:]), r=[ident_k], w=[ident_k])
    for mk, cs in ((mask64, 64), (mask32, 32)):
        R.op("pool", lambda e, mk=mk: e.affine_select(out=mk[:], in_=onesf[:], pattern=[[1, 128]], compare_op=ALU.is_ge,
                                                      fill=0.0, base=0, channel_multiplier=-1), r=[ident_k], w=[mask_k])
        for i in range(128 // cs - 1):
            R.op("pool", lambda e, mk=mk, i=i, cs=cs: e.memset(mk[cs * i:cs * (i + 1), cs * (i + 1):128], 0.0), w=[mask_k])
    for rm, cs in ((rm64, 64), (rm32, 32)):
        R.op("pool", lambda e, rm=rm: e.memset(rm[:], 1.0), w=[rm_k])
        R.op("pool", lambda e, rm=rm, cs=cs: e.memset(rm[:].rearrange("p (c s) -> p c s", s=cs)[:, :, 0:1], 0.0), w=[rm_k])
    R.dma("act", lbt[:], lb_logits.rearrange("r (h p) -> p r h", p=128), sem="c0", w=[lb_k], allow_slow_non_contiguous=True)
    R.dma("act", hgT[:], hg_g.rearrange("(h p) -> p h", p=128), sem="c1", w=[vec_k], allow_slow_non_contiguous=True)
    R.dma("act", pscT[:], p_scale.rearrange("(h p) -> p h", p=128), sem="c1", w=[vec_k], allow_slow_non_contiguous=True)
    R.dma("act", metat[:], meta, sem="c2", w=[meta_k])
    R.dma("pool", wpm[:], w_pm.rearrange("g c d -> c g d"), sem="c3", w=[wpm_k])
    R.op("dve", lambda e: e.tensor_tensor(out=oml[:], in0=lbt[:, 0, :], in1=lbt[:, 1, :], op=ALU.subtract), r=[lb_k], w=[lb_k])
    R.op("act", lambda e: e.activation(out=oml[:], in_=oml[:], func=AF.Exp), r=[lb_k], w=[lb_k])
    R.op("act", lambda e: e.activation(out=oml[:], in_=oml[:], func=AF.Ln, bias=cst[:, 1:2], scale=1.0), r=[lb_k, cst_k], w=[lb_k])
    R.op("act", lambda e: e.activation(out=oml[:], in_=oml[:], func=AF.Exp, scale=-1.0), r=[lb_k], w=[lb_k])
    R.op("dve", lambda e: e.tensor_scalar(out=noml[:], in0=oml[:], scalar1=-1.0, scalar2=None, op0=ALU.mult), r=[lb_k], w=[lb_k])
    R.op("pool", lambda e: e.iota(out=iot[:], pattern=[[1, 16]], base=1, channel_multiplier=0), w=[invc_k])
    R.op("pool", lambda e: e.tensor_copy(out=iof[:], in_=iot[:]), r=[invc_k], w=[invc_k])
    for g in range(4):
        R.op("dve", lambda e, g=g: e.tensor_scalar(out=invc[:, g, :], in0=iof[:], scalar1=metat[:, 1:2], scalar2=float(2 << g),
                                                   op0=ALU.add, op1=ALU.min), r=[invc_k, meta_k], w=[invc_k])
    R.op("dve", lambda e: e.reciprocal(out=invc[:], in_=invc[:]), r=[invc_k], w=[invc_k])
    for h in range(8):
        R.op("pool", lambda e, h=h: e.memset(Sst[:, 0, h, :], 0.0), w=[Sst_k[0][h]])
    for g in range(4):
        R.op("pool", lambda e, g=g: e.memset(carry[:, g, :], 0.0), w=[carry_k[g]])

    for i in range(NTMP):
        R.op("pool", lambda e, i=i: e.memset(tmp[:, i, :], 0.0), w=[tmp_k[i]])

    if do_sample:
        for j in range(4):
            R.dma("act", Eb[0:15, j, :], s_pool[j], sem="c5", w=[Eb_k[j]])
        pst, pk = PS()
        for j in range(4):
            for g in range(4):
                last = (j == 3 and g == 3)
                R.op("pe", lambda e, j=j, g=g, pst=pst: e.transpose(out=pst[:, (g * 4 + j) * 15:(g * 4 + j) * 15 + 15],
                                                           in_=Eb[0:15, j, g * 128:(g + 1) * 128], identity=identf[0:15, 0:15]),
                     r=[Eb_k[j], ident_k], w=[pk], inc=last)
        R.op("dve", lambda e, pst=pst: e.tensor_copy(out=spre[:].rearrange("p g j r -> p (g j r)"), in_=pst[:, 0:240]), r=[pk], w=[spre_k])

    LNP = {"in": (ln_in_g, ln_in_b), "1": (ln1_g, ln1_b), "2": (ln2_g, ln2_b)}

    def load_ln(name):
        if state["ln"] == name:
            return
        state["ln"] = name
        g_ap, b_ap = LNP[name]
        R.dma("pool", lnG[:], g_ap.partition_broadcast(128), sem="lng", w=[lnG_k])
        R.dma("pool", lnB[:], b_ap.partition_broadcast(128), sem="lnb", w=[lnB_k])

    ln_st = {}

    def ln_stats(slot):
        x = xt[:, slot, :]
        k = xt_k[slot]
        si = state["st"] % NST
        state["st"] += 1
        sk = st_k[si]
        R.op("dve", lambda e: e.bn_stats(out=st[:, si, 0:6], in_=xt[:, slot, 0:512]), r=[k], w=[sk])
        R.op("dve", lambda e: e.bn_stats(out=st[:, si, 6:12], in_=xt[:, slot, 512:1024]), r=[k], w=[sk])
        R.op("dve", lambda e: e.bn_aggr(out=mv[:, si, :], in_=st[:, si, :]), r=[sk], w=[sk])
        R.op("act", lambda e: e.activation(out=rs[:, si, 0:1], in_=mv[:, si, 1:2], func=AF.Ln, bias=cst[:, 0:1], scale=1.0),
             r=[sk, cst_k], w=[sk])
        R.op("act", lambda e: e.activation(out=rs[:, si, 1:2], in_=rs[:, si, 0:1], func=AF.Exp, scale=-0.5), r=[sk], w=[sk])
        R.op("dve", lambda e: e.scalar_tensor_tensor(out=rs[:, si, 2:3], in0=mv[:, si, 0:1], scalar=-1.0, in1=rs[:, si, 1:2],
                                                     op0=ALU.mult, op1=ALU.mult), r=[sk], w=[sk])
        R.op("act", lambda e: e.activation(out=x, in_=x, func=AF.Identity, bias=rs[:, si, 2:3], scale=rs[:, si, 1:2]),
             r=[sk, k], w=[k])

    def ln_affine(slot, eng="pool"):
        x = xt[:, slot, :]
        k = xt_k[slot]
        R.op(eng, lambda e: e.tensor_tensor(out=x, in0=x, in1=lnG[:], op=ALU.mult), r=[k, lnG_k], w=[k])
        R.op(eng, lambda e: e.tensor_tensor(out=x, in0=x, in1=lnB[:], op=ALU.add), r=[k, lnB_k], w=[k])

    def ln_cast(slot, bf_slot):
        R.op("act", lambda e: e.activation(out=xb[:, bf_slot, :], in_=xt[:, slot, :], func=AF.Copy), r=[xt_k[slot]], w=[xb_k[bf_slot]])

    def transpose_to_xT(sub, bf_slot):
        pst, pk = PS()
        psb = pst.bitcast(BF16).rearrange("p (b t) -> p b t", b=8)
        for blk in range(8):
            R.op("pe", lambda e, blk=blk: e.transpose(out=psb[:, blk, :], in_=xb[:, bf_slot, blk * 128:(blk + 1) * 128],
                                                      identity=identb[:]), r=[xb_k[bf_slot], ident_k], w=[pk], inc=(blk == 7))
        R.op("dve", lambda e: e.tensor_copy(out=xT[:, :, sub * 128:(sub + 1) * 128], in_=psb), r=[pk], w=[xT_k[sub]])

    def fm_proj(slab, col0, rhsT, rhs_k, ntok, nk=8):
        sl, slk = slab
        pst, pk = PS()
        for k in range(nk):
            R.op("pe", lambda e, k=k: e.matmul(pst[:, 0:ntok], lhsT=sl[:, k * 512 + col0:k * 512 + col0 + 128],
                                               rhs=rhsT[:, k, 0:ntok], start=(k == 0), stop=(k == nk - 1)),
                 r=[slk] + rhs_k, w=[pk], inc=(k == nk - 1))
        return pst[:, 0:ntok], pk


    def fblock_a(blk, ps_ap, pk, ntok, CS, main):
        rm = rm64 if CS == 64 else rm32
        t0, k0 = TMP(); t1, k1 = TMP(); t2, k2 = TMP()
        a0 = t0[:, 0:ntok]; a1 = t1[:, 0:ntok]; a2 = t2[:, 0:ntok]
        R.op("act", lambda e: e.activation(out=a0, in_=ps_ap, func=AF.Exp), r=[pk], w=[k0])
        R.op("act", lambda e: e.activation(out=a0, in_=a0, func=AF.Ln, bias=cst[:, 1:2], scale=1.0), r=[k0, cst_k], w=[k0])
        R.op("act", lambda e: e.activation(out=a0, in_=a0, func=AF.Exp, scale=-1.0), r=[k0], w=[k0])
        R.op("act", lambda e: e.activation(out=a1, in_=a0, func=AF.Ln, bias=cst[:, 1:2], scale=noml[:, blk:blk + 1]),
             r=[k0, cst_k, lb_k], w=[k1])
        R.op("dve", lambda e: e.tensor_scalar(out=a0, in0=a0, scalar1=oml[:, blk:blk + 1], scalar2=None, op0=ALU.mult),
             r=[k0, lb_k], w=[k0])
        R.op("dve", lambda e: e.tensor_tensor_scan(out=a2, data0=rm[:, 0:ntok], data1=a1, initial=0.0, op0=ALU.mult, op1=ALU.add),
             r=[k1, rm_k], w=[k2])
        return (blk, a0, a1, a2, k0, k1, k2, ntok, CS, main)

    def fblock_b(ctx):
        blk, a0, a1, a2, k0, k1, k2, ntok, CS, main = ctx
        NCH = ntok // CS
        b3 = a2.rearrange("p (c s) -> p c s", s=CS)
        R.op("act", lambda e: e.activation(out=elast[:, blk, 0:NCH], in_=b3[:, :, CS - 1], func=AF.Exp), r=[k2], w=[elast_k[blk]])
        if main:
            R.op("act", lambda e: e.activation(out=Eb[:, blk % 4, 0:ntok], in_=a2, func=AF.Exp), r=[k2], w=[Eb_k[blk % 4]])
        R.op("act", lambda e: e.activation(out=a1, in_=a2, func=AF.Exp, scale=-1.0), r=[k2, k1], w=[k1])
        pe_ = "pool" if main else "dve"
        R.op(pe_, lambda e: e.tensor_tensor(out=a1, in0=a0, in1=a1, op=ALU.mult), r=[k0, k1], w=[k1])
        if main:
            R.op("pool", lambda e: e.tensor_copy(out=big[:, 8 + blk, 0:ntok], in_=a1), r=[k1], w=[big_k[8 + blk]])
        R.op(pe_, lambda e: e.tensor_tensor(out=big[:, 16 + blk, 0:ntok].rearrange("p (c s) -> p c s", s=CS),
                                               in0=a1.rearrange("p (c s) -> p c s", s=CS),
                                               in1=elast[:, blk, 0:NCH].unsqueeze(2).broadcast_to([128, NCH, CS]), op=ALU.mult),
             r=[k1, elast_k[blk]], w=[big_k[16 + blk]])

    def qblock(blk, ps_ap, pk, ntok):
        R.op("dve", lambda e: e.tensor_tensor(out=big[:, blk, 0:ntok], in0=ps_ap, in1=Eb[:, blk % 4, 0:ntok], op=ALU.mult),
             r=[pk, Eb_k[blk % 4]], w=[big_k[blk]])

    def pool_group(g, ps_ap, pk, ntok, nseq, sample, first_main, save_carry):
        L = ntok // nseq
        E = 15 + L
        w = 2 << g
        ex, exk = TMPW(); e2_, e2k = TMPW(); e3_, e3k = TMPW()
        ev = ex[:, 0:nseq * E].rearrange("p (j e) -> p j e", j=nseq)
        e2 = e2_[:, 0:nseq * E].rearrange("p (j e) -> p j e", j=nseq)
        e3 = e3_[:, 0:nseq * E].rearrange("p (j e) -> p j e", j=nseq)
        R.op("act", lambda e: e.activation(out=ev[:, :, 15:E], in_=ps_ap.rearrange("p (j l) -> p j l", j=nseq), func=AF.Copy),
             r=[pk], w=[exk])
        if sample:
            R.op("dve", lambda e: e.tensor_copy(out=ev[:, :, 0:15], in_=spre[:, g, :, :]), r=[spre_k], w=[exk])
        else:
            R.op("dve", lambda e: e.tensor_copy(out=ev[:, 0, 0:15], in_=carry[:, g, :]), r=[carry_k[g]], w=[exk])
        src, sk = ev, exk
        for si_ in range(g + 1):
            sh = 1 << si_
            dst, dk = (e2, e2k) if si_ % 2 == 0 else (e3, e3k)
            R.op("dve", lambda e, dst=dst, src=src, sh=sh: e.tensor_tensor(out=dst[:, :, sh:E], in0=src[:, :, sh:E],
                                                                           in1=src[:, :, 0:E - sh], op=ALU.add),
                 r=[sk], w=[dk])
            src, sk = dst, dk
        outv = pooledT[:, g, 0:ntok].rearrange("p (j l) -> p j l", j=nseq)
        R.op("dve", lambda e, src=src: e.scalar_tensor_tensor(out=outv, in0=src[:, :, 15:E], scalar=1.0 / w, in1=ev[:, :, 15:E],
                                                              op0=ALU.mult, op1=ALU.subtract), r=[sk, exk], w=[pooled_k[g]])
        if first_main:
            tt_, tk_ = TMP()
            R.op("dve", lambda e, src=src: e.tensor_tensor(out=tt_[:, 0:16], in0=src[:, 0, 15:31], in1=invc[:, g, :], op=ALU.mult),
                 r=[sk, invc_k], w=[tk_])
            R.op("dve", lambda e: e.tensor_tensor(out=pooledT[:, g, 0:16], in0=tt_[:, 0:16], in1=ev[:, 0, 15:31], op=ALU.subtract),
                 r=[tk_, exk], w=[pooled_k[g]])
        if save_carry:
            R.op("dve", lambda e: e.tensor_copy(out=carry[:, g, :], in_=ev[:, 0, L:L + 15]), r=[exk], w=[carry_k[g]])

    def hgrn_stage1(sub, CS, need_A):
        c0 = sub * 128
        NC2 = 128 // CS
        res = {}
        if need_A:
            a = sub % 2
            mk = mask64 if CS == 64 else mask32
            for hg in range(2):
                pst, pk = PS()
                for hh in range(4):
                    h = hg * 4 + hh
                    R.op("pe", lambda e, h=h, hh=hh, pst=pst: e.matmul(pst[:, hh * 128:(hh + 1) * 128], lhsT=big[:, 8 + h, c0:c0 + 128],
                                                              rhs=big[:, h, c0:c0 + 128], start=True, stop=True),
                         r=[big_k[8 + h], big_k[h]], w=[pk], inc=(hh == 3))
                R.op("dve", lambda e, hg=hg, pst=pst: e.tensor_tensor(
                    out=ATm2[:, a, hg * 4:(hg + 1) * 4, :], in0=pst.rearrange("p (h t) -> p h t", h=4),
                    in1=mk[:].unsqueeze(1).broadcast_to([128, 4, 128]), op=ALU.mult), r=[pk, mask_k], w=[ATm2_k[a][hg]])
        a2 = 0
        pst, pk = PS()
        psb = pst.bitcast(BF16).rearrange("p (b t) -> p b t", b=8)
        for h in range(8):
            R.op("pe", lambda e, h=h: e.transpose(out=psb[:, h, :], in_=big[:, 16 + h, c0:c0 + 128], identity=identb[:]),
                 r=[big_k[16 + h], ident_k], w=[pk], inc=(h == 7))
        R.op("act", lambda e: e.activation(out=khat[:, a2, :, :], in_=psb, func=AF.Copy), r=[pk], w=[khat_k[a2]])
        dS = []
        for c in range(NC2):
            row = []
            for hg in range(2):
                pst, pk = PS()
                for hh in range(4):
                    h = hg * 4 + hh
                    kw = {"tile_position": (96, 0)} if c * CS == 96 else {}
                    R.op("pe", lambda e, h=h, hh=hh, c=c, pst=pst, kw=kw: e.matmul(
                        pst[:, hh * 128:(hh + 1) * 128], lhsT=khat[c * CS:(c + 1) * CS, a2, h, :],
                        rhs=vtm[c * CS:(c + 1) * CS, sub, h * 128:(h + 1) * 128], start=True, stop=True, **kw),
                         r=[khat_k[a2], vtm_k[sub]], w=[pk], inc=(hh == 3))
                row.append((pst, pk))
            dS.append(row)
        res["dS"] = dS
        return res

    ATm2 = sb("ATm2", [128, 2, 8, 128], BF16); ATm2_k = [[Tok("ATm2_%d_%d" % (a, b)) for b in range(2)] for a in range(2)]

    def state_update(cur, h, ech, dS_ps, dS_k, hh, nxt=None):
        dst = cur if nxt is None else nxt
        R.op("dve", lambda e: e.scalar_tensor_tensor(out=Sst[:, dst, h, :], in0=Sst[:, cur, h, :], scalar=ech,
                                                     in1=dS_ps[:, hh * 128:(hh + 1) * 128], op0=ALU.mult, op1=ALU.add),
             r=[Sst_k[cur][h], dS_k, elast_k[h]], w=[Sst_k[dst][h]])

    def state_update_all(cur, ch, banks):
        R.op("dve", lambda e: e.tensor_tensor(out=Sst[:, cur, :, :], in0=Sst[:, cur, :, :],
                                              in1=elast[:, :, ch:ch + 1].to_broadcast([128, 8, 128]), op=ALU.mult),
             r=Sst_k[cur] + elast_k, w=Sst_k[cur])
        for hg in range(2):
            pst, pk = banks[hg]
            R.op("dve", lambda e, hg=hg, pst=pst: e.tensor_tensor(out=Sst[:, cur, hg * 4:(hg + 1) * 4, :],
                                                                  in0=pst.rearrange("p (h v) -> p h v", h=4),
                                                                  in1=Sst[:, cur, hg * 4:(hg + 1) * 4, :], op=ALU.add),
                 r=[pk] + Sst_k[cur][hg * 4:(hg + 1) * 4], w=Sst_k[cur][hg * 4:(hg + 1) * 4])

    def sbv(i):
        if i < NSB:
            return Sb[:, i, :, :], Sb_k[i]
        return Eb[:, i - NSB, :].bitcast(BF16).rearrange("p (h v) -> p h v", h=8), Eb_k[i - NSB]

    def cast_state(cur):
        i = state["sbv"] % (NSB + 4)
        state["sbv"] += 1
        ap, tk = sbv(i)
        R.op("act", lambda e: e.activation(out=ap, in_=Sst[:, cur, :, :], func=AF.Copy), r=Sst_k[cur], w=[tk])
        return i

    def hgrn_A(sub, CS):
        c0 = sub * 128
        a = sub % 2
        mk = mask64 if CS == 64 else mask32
        for hg in range(2):
            pst, pk = PSP("hA")
            for hh in range(4):
                h = hg * 4 + hh
                R.op("pe", lambda e, h=h, hh=hh, pst=pst: e.matmul(pst[:, hh * 128:(hh + 1) * 128], lhsT=big[:, 8 + h, c0:c0 + 128],
                                                                   rhs=big[:, h, c0:c0 + 128], start=True, stop=True),
                     r=[big_k[8 + h], big_k[h]], w=[pk], inc=(hh == 3))
            R.op("dve", lambda e, hg=hg, pst=pst: e.tensor_tensor(
                out=ATm2[:, a, hg * 4:(hg + 1) * 4, :], in0=pst.rearrange("p (h t) -> p h t", h=4),
                in1=mk[:].unsqueeze(1).broadcast_to([128, 4, 128]), op=ALU.mult), r=[pk, mask_k], w=[ATm2_k[a][hg]])

    sq4 = Sst[:, 1, :, :].rearrange("p h v -> p (h v)").bitcast(BF16).rearrange("p (s t) -> p s t", s=4)
    obank = {}

    def hgrn_oA(sub, CS, sbv_list):
        c0 = sub * 128
        a = sub % 2
        NC2 = 128 // CS
        for hg in range(2):
            pst, pk = PSP("hO")
            for hh in range(4):
                h = hg * 4 + hh
                o_ps = pst[:, hh * 128:(hh + 1) * 128]
                R.op("pe", lambda e, h=h, o_ps=o_ps: e.matmul(o_ps, lhsT=vtm[:, sub, h * 128:(h + 1) * 128], rhs=ATm2[:, a, h, :],
                                                              start=True, stop=False, skip_group_check=True),
                     r=[vtm_k[sub], ATm2_k[a][hg]], w=[pk], inc=False)
                for c in range(NC2):
                    sap, stk = sbv(sbv_list[c])
                    R.op("pe", lambda e, h=h, c=c, sap=sap, o_ps=o_ps: e.matmul(
                        o_ps[:, c * CS:(c + 1) * CS], lhsT=sap[:, h, :], rhs=big[:, h, c0 + c * CS:c0 + (c + 1) * CS],
                        start=False, stop=(c == NC2 - 1), skip_group_check=True),
                         r=[stk, big_k[h]], w=[pk], inc=(hh == 3 and c == NC2 - 1))
            si = (sub % 2) * 2 + hg
            sqk = [Sst_k[1][2 * si], Sst_k[1][2 * si + 1]]
            R.op("act", lambda e, pst=pst, si=si: e.activation(out=sq4[:, si, :], in_=pst, func=AF.Square), r=[pk], w=sqk)
            obank[(sub, hg)] = (pst, pk, si, sqk)

    def hgrn_oB(sub):
        c0 = sub * 128
        for hg in range(2):
            pst, pk, si, sqk = obank.pop((sub, hg))
            ps_reserved.discard(ps_k.index(pk))
            ms, mk_ = PSP("hA")
            R.op("pe", lambda e, ms=ms, si=si: e.matmul(ms, lhsT=onesb[:], rhs=sq4[:, si, :], start=True, stop=True),
                 r=sqk + [ident_k], w=[mk_])
            t0, k0 = TMP()
            R.op("act", lambda e, ms=ms, t0=t0: e.activation(out=t0, in_=ms, func=AF.Ln, bias=cst[:, 2:3], scale=1.0),
                 r=[mk_, cst_k], w=[k0])
            R.op("act", lambda e, t0=t0: e.activation(out=t0, in_=t0, func=AF.Exp, scale=-0.5), r=[k0], w=[k0])
            R.op("dve", lambda e, pst=pst, t0=t0: e.tensor_tensor(out=t0, in0=pst, in1=t0, op=ALU.mult), r=[pk, k0], w=[k0])
            R.op("dve", lambda e, hg=hg, t0=t0: e.tensor_tensor(
                out=oT[:, hg * 4:(hg + 1) * 4, c0:c0 + 128], in0=t0.rearrange("p (h t) -> p h t", h=4),
                in1=sgT[:, hg * 4:(hg + 1) * 4, c0:c0 + 128], op=ALU.mult),
                 r=[k0] + sgT_k[hg * 4:(hg + 1) * 4], w=[oT_k[sub]])

    def hgrn_out(sub, CS, sbv_list):
        c0 = sub * 128
        a = sub % 2
        NC2 = 128 // CS
        for hg in range(2):
            pst, pk = PS()
            for hh in range(4):
                h = hg * 4 + hh
                o_ps = pst[:, hh * 128:(hh + 1) * 128]
                R.op("pe", lambda e, h=h, o_ps=o_ps: e.matmul(o_ps, lhsT=vtm[:, sub, h * 128:(h + 1) * 128], rhs=ATm2[:, a, h, :],
                                                              start=True, stop=False, skip_group_check=True),
                     r=[vtm_k[sub], ATm2_k[a][hg]], w=[pk], inc=False)
                for c in range(NC2):
                    sap, stk = sbv(sbv_list[c])
                    R.op("pe", lambda e, h=h, c=c, sap=sap, o_ps=o_ps: e.matmul(
                        o_ps[:, c * CS:(c + 1) * CS], lhsT=sap[:, h, :], rhs=big[:, h, c0 + c * CS:c0 + (c + 1) * CS],
                        start=False, stop=(c == NC2 - 1), skip_group_check=True),
                         r=[stk, big_k[h]], w=[pk], inc=(hh == 3 and c == NC2 - 1))
            sa = 0
            R.op("act", lambda e, pst=pst, sa=sa: e.activation(out=sq[:, sa, :], in_=pst, func=AF.Square), r=[pk], w=[sq_k[sa]])
            ms, mk_ = PS()
            R.op("pe", lambda e, ms=ms, sa=sa: e.matmul(ms, lhsT=onesb[:], rhs=sq[:, sa, :], start=True, stop=True),
                 r=[sq_k[sa], ident_k], w=[mk_])
            t0, k0 = TMP()
            R.op("act", lambda e, ms=ms, t0=t0: e.activation(out=t0, in_=ms, func=AF.Ln, bias=cst[:, 2:3], scale=1.0),
                 r=[mk_, cst_k], w=[k0])
            R.op("act", lambda e, t0=t0: e.activation(out=t0, in_=t0, func=AF.Exp, scale=-0.5), r=[k0], w=[k0])
            R.op("dve", lambda e, pst=pst, t0=t0: e.tensor_tensor(out=t0, in0=pst, in1=t0, op=ALU.mult), r=[pk, k0], w=[k0])
            R.op("dve", lambda e, hg=hg, t0=t0: e.tensor_tensor(
                out=oT[:, hg * 4:(hg + 1) * 4, c0:c0 + 128], in0=t0.rearrange("p (h t) -> p h t", h=4),
                in1=sgT[:, hg * 4:(hg + 1) * 4, c0:c0 + 128], op=ALU.mult),
                 r=[k0] + sgT_k[hg * 4:(hg + 1) * 4], w=[oT_k[sub]])

    def tile_load(xsrc, psrc, row0, NS, xs):
        for sub in range(NS):
            R.dma("sp", xt[:, xs + sub, :], xsrc[row0 + sub * 128:row0 + (sub + 1) * 128, :], sem="x%d" % (xs + sub), w=[xt_k[xs + sub]])
        if psrc is not None:
            for sub in range(NS):
                R.dma("sp", pt[:, sub, :], psrc[row0 + sub * 128:row0 + (sub + 1) * 128, :], sem="p%d" % sub, w=[pt_k[sub]])

    def front_A(NS, xs, eng="pool"):
        load_ln("in")
        for sub in range(NS):
            ln_stats(xs + sub)
        for sub in range(NS):
            ln_affine(xs + sub, eng)

    def front_B(NS, xs, with_p):
        for sub in range(NS):
            ln_cast(xs + sub, sub % 2)
            transpose_to_xT(sub, sub % 2)
            if with_p:
                R.op("pool", lambda e, sub=sub: e.tensor_copy(out=pb[:], in_=pt[:, sub, :]), r=[pt_k[sub]], w=[pb_k])
                pst, pk = PS()
                psb = pst.bitcast(BF16).rearrange("p (b t) -> p b t", b=8)
                for blk in range(2):
                    R.op("pe", lambda e, blk=blk, psb=psb: e.transpose(out=psb[:, blk, :], in_=pb[:, blk * 128:(blk + 1) * 128],
                                                                       identity=identb[:]), r=[pb_k, ident_k], w=[pk], inc=(blk == 1))
                R.op("act", lambda e, sub=sub, psb=psb: e.activation(out=pT[:, :, sub * 128:(sub + 1) * 128], in_=psb[:, 0:2, :],
                                                                     func=AF.Copy), r=[pk], w=[pT_k[sub]])

    def proj_f(name, half, ntok, CS, NS, main):
        slab = next_slab(name)
        prev = None
        for j in range(4):
            ps_ap, pk = fm_proj(slab, j * 128, xT, xT_k[0:NS], ntok)
            ctx = fblock_a(half * 4 + j, ps_ap, pk, ntok, CS, main)
            if prev is not None:
                fblock_b(prev)
            prev = ctx
        fblock_b(prev)

    def proj_q(name, half, ntok, NS):
        slab = next_slab(name)
        for j in range(4):
            ps_ap, pk = fm_proj(slab, j * 128, xT, xT_k[0:NS], ntok)
            qblock(half * 4 + j, ps_ap, pk, ntok)

    def proj_i(name, half, NS):
        sl, slk = next_slab(name)
        for sub in range(NS):
            pst, pk = PS()
            for k in range(8):
                R.op("pe", lambda e, k=k, sub=sub, pst=pst: e.matmul(pst, lhsT=xT[:, k, sub * 128:(sub + 1) * 128],
                                                                     rhs=sl[:, k * 512:(k + 1) * 512], start=(k == 0), stop=(k == 7)),
                     r=[slk, xT_k[sub]], w=[pk], inc=(k == 7))
            R.op("act", lambda e, sub=sub, pst=pst: e.activation(out=vtm[:, sub, half * 512:(half + 1) * 512], in_=pst, func=AF.Copy),
                 r=[pk], w=[vtm_k[sub]])

    def proj_vpool(ntok, nseq, NS, sample, first_main, save_carry, pool_dst):
        slab = next_slab("v")
        sl, slk = slab
        for g in range(4):
            ps_ap, pk = fm_proj(slab, g * 128, xT, xT_k[0:NS], ntok)
            pool_group(g, ps_ap, pk, ntok, nseq, sample, first_main, save_carry)
        if pool_dst is not None:
            L = ntok // nseq
            for j in range(nseq):
                pst, pk = PS()
                t0_ = j * L + L - 15
                for k in range(8):
                    R.op("pe", lambda e, k=k, pst=pst, t0_=t0_: e.matmul(pst[0:15, :], lhsT=xT[:, k, t0_:t0_ + 15],
                                                                         rhs=sl[:, k * 512:(k + 1) * 512], start=(k == 0), stop=(k == 7)),
                         r=[slk] + xT_k[0:NS], w=[pk], inc=(k == 7))
                tq, tqk = TMP()
                R.op("act", lambda e, tq=tq, pst=pst: e.activation(out=tq[0:15, :], in_=pst[0:15, :], func=AF.Copy),
                     r=[pk], w=[tqk])
                R.dma("act", pool_dst[j], tq[0:15, :], sem="po", r=[tqk])

    def pm_proj(ntok):
        for g in range(4):
            pst, pk = PS()
            R.op("pe", lambda e, g=g, pst=pst: e.matmul(pst[:, 0:ntok], lhsT=wpm[:, g, :], rhs=pooledT[:, g, 0:ntok], start=True, stop=True),
                 r=[wpm_k, pooled_k[g]], w=[pk])
            R.op("dve", lambda e, g=g, pst=pst: e.tensor_scalar(out=pmT[:, g, 0:ntok], in0=pst[:, 0:ntok], scalar1=pscT[:, g:g + 1],
                                                                scalar2=None, op0=ALU.mult), r=[pk, vec_k], w=[pmT_k[g]])

    def proj_g(name, half, ntok, NS):
        slab = next_slab(name)
        for j in range(4):
            blk = half * 4 + j
            ps_ap, pk = fm_proj(slab, j * 128, xT, xT_k[0:NS], ntok)
            t0, k0 = TMP()
            R.op("act", lambda e, t0=t0, ps_ap=ps_ap: e.activation(out=t0[:, 0:ntok], in_=ps_ap, func=AF.Silu), r=[pk], w=[k0])
            R.op("dve", lambda e, t0=t0, blk=blk: e.tensor_scalar(out=sgT[:, blk, 0:ntok], in0=t0[:, 0:ntok], scalar1=hgT[:, blk:blk + 1],
                                                                   scalar2=None, op0=ALU.mult), r=[k0, vec_k], w=[sgT_k[blk]])

    def merge_phase(ntok, NS, xs):
        for h in range(2):
            for nm, dst, dk in (("ga%d" % h, gaS, gaS_k), ("gb%d" % h, gbS, gbS_k)):
                slab = next_slab(nm)
                for j in range(4):
                    ps_ap, pk = fm_proj(slab, j * 128, xT, xT_k[0:NS], ntok)
                    R.op("act", lambda e, dst=dst, j=j, ps_ap=ps_ap: e.activation(out=dst[:, j, 0:ntok], in_=ps_ap, func=AF.Sigmoid),
                         r=[pk], w=[dk[j]])
            slab_b = next_slab("wb%d" % h)
            slab_a = next_slab("wa%d" % h)
            for j in range(4):
                yb, ybk = fm_proj(slab_b, j * 128, pmT, pmT_k, ntok, nk=4)
                ya, yak = fm_proj(slab_a, j * 128, oT, oT_k[0:NS], ntok)
                t2, k2 = TMP(); t1, k1 = TMP()
                R.op("dve", lambda e, t2=t2, yb=yb, j=j: e.tensor_tensor(out=t2[:, 0:ntok], in0=yb, in1=gbS[:, j, 0:ntok], op=ALU.mult),
                     r=[ybk, gbS_k[j]], w=[k2])
                R.op("dve", lambda e, t1=t1, ya=ya, j=j: e.tensor_tensor(out=t1[:, 0:ntok], in0=ya, in1=gaS[:, j, 0:ntok], op=ALU.mult),
                     r=[yak, gaS_k[j]], w=[k1])
                R.op("pool", lambda e, t1=t1, t2=t2, h=h, j=j: e.tensor_tensor(out=big[:, h * 4 + j, 0:ntok], in0=t1[:, 0:ntok],
                                                                               in1=t2[:, 0:ntok], op=ALU.add),
                     r=[k1, k2], w=[big_k[h * 4 + j]])
        sl0, slk0 = next_slab("wo0")
        sl1, slk1 = next_slab("wo1")
        load_ln("1")
        for sub in range(NS):
            for h2, (sl, slk) in enumerate(((sl0, slk0), (sl1, slk1))):
                pst, pk = PS()
                for k in range(8):
                    R.op("pe", lambda e, k=k, sub=sub, pst=pst, sl=sl: e.matmul(pst, lhsT=big[:, k, sub * 128:(sub + 1) * 128],
                                                                                rhs=sl[:, k * 512:(k + 1) * 512], start=(k == 0), stop=(k == 7)),
                         r=[slk, big_k[k]], w=[pk], inc=(k == 7))
                R.op("dve", lambda e, sub=sub, pst=pst, h2=h2: e.scalar_tensor_tensor(
                    out=xt[:, xs + sub, h2 * 512:(h2 + 1) * 512], in0=xt[:, xs + sub, h2 * 512:(h2 + 1) * 512], scalar=ALPHA, in1=pst,
                    op0=ALU.mult, op1=ALU.add), r=[pk, xt_k[xs + sub]], w=[xt_k[xs + sub]])
            ln_stats(xs + sub)
        for sub in range(NS):
            ln_affine(xs + sub)
        for sub in range(NS):
            ln_cast(xs + sub, sub % 2)
            transpose_to_xT(sub, sub % 2)

    def ffn_phase(ntok, NS, ydst, row0, xs, mid_hook=None):
        for s in range(11):
            sl, slk = next_slab("up%d" % s)
            for jj in range(2):
                j = 2 * s + jj
                gps, gk = PS()
                ups, uk = PS()
                for k in range(8):
                    R.op("pe", lambda e, k=k, jj=jj, gps=gps, sl=sl: e.matmul(gps[:, 0:ntok], lhsT=sl[:, k * 256 + jj * 128:k * 256 + jj * 128 + 128],
                                                                       rhs=xT[:, k, 0:ntok], start=(k == 0), stop=(k == 7)),
                         r=[slk] + xT_k[0:NS], w=[gk], inc=(k == 7))
                for k in range(8):
                    R.op("pe", lambda e, k=k, jj=jj, ups=ups, sl=sl: e.matmul(ups[:, 0:ntok], lhsT=sl[:, 2048 + k * 256 + jj * 128:2048 + k * 256 + jj * 128 + 128],
                                                                       rhs=xT[:, k, 0:ntok], start=(k == 0), stop=(k == 7)),
                         r=[slk] + xT_k[0:NS], w=[uk], inc=(k == 7))
                t0, k0 = TMP()
                R.op("act", lambda e, t0=t0, gps=gps: e.activation(out=t0[:, 0:ntok], in_=gps[:, 0:ntok], func=AF.Silu), r=[gk], w=[k0])
                R.op("dve", lambda e, t0=t0, ups=ups, j=j: e.tensor_tensor(out=big[:, j, 0:ntok], in0=t0[:, 0:ntok], in1=ups[:, 0:ntok], op=ALU.mult),
                     r=[k0, uk], w=[big_k[j]])
        load_ln("2")
        for h in range(2):
            sl, slk = next_slab("pg%d" % h)
            slp, slpk = next_slab("pp%d" % h)
            for sub in range(NS):
                gps, gk = PS()
                pps, pk = PS()
                for k in range(8):
                    R.op("pe", lambda e, k=k, sub=sub, gps=gps, sl=sl: e.matmul(gps, lhsT=xT[:, k, sub * 128:(sub + 1) * 128],
                                                                                rhs=sl[:, k * 512:(k + 1) * 512], start=(k == 0), stop=(k == 7)),
                         r=[slk, xT_k[sub]], w=[gk], inc=(k == 7))
                for k in range(2):
                    R.op("pe", lambda e, k=k, sub=sub, pps=pps, slp=slp: e.matmul(pps, lhsT=pT[:, k, sub * 128:(sub + 1) * 128],
                                                                                  rhs=slp[:, k * 512:(k + 1) * 512], start=(k == 0), stop=(k == 1)),
                         r=[slpk, pT_k[sub]], w=[pk], inc=(k == 1))
                t0, k0 = TMP()
                R.op("act", lambda e, t0=t0, gps=gps: e.activation(out=t0, in_=gps, func=AF.Sigmoid), r=[gk], w=[k0])
                R.op("dve", lambda e, t0=t0, pps=pps: e.tensor_tensor(out=t0, in0=t0, in1=pps, op=ALU.mult), r=[k0, pk], w=[k0])
                R.op("dve", lambda e, t0=t0, sub=sub, h=h: e.scalar_tensor_tensor(
                    out=xt[:, xs + sub, h * 512:(h + 1) * 512], in0=xt[:, xs + sub, h * 512:(h + 1) * 512], scalar=ALPHA, in1=t0,
                    op0=ALU.mult, op1=ALU.add), r=[k0, xt_k[xs + sub]], w=[xt_k[xs + sub]])
            if h == 1 and mid_hook is not None:
                mid_hook()
            accs = [PS() for _ in range(NS)]
            for kg in range(3):
                nk = 8 if kg < 2 else 6
                sl, slk = next_slab("dn%d%d" % (h, kg))
                for sub in range(NS):
                    aps, ak = accs[sub]
                    for kk in range(nk):
                        k = kg * 8 + kk
                        R.op("pe", lambda e, kk=kk, k=k, sub=sub, aps=aps, sl=sl: e.matmul(aps, lhsT=big[:, k, sub * 128:(sub + 1) * 128],
                                                                                           rhs=sl[:, kk * 512:(kk + 1) * 512], start=(k == 0), stop=(k == 21)),
                             r=[slk, big_k[k]], w=[ak], inc=(kk == nk - 1))
            for sub in range(NS):
                aps, ak = accs[sub]
                R.op("dve", lambda e, sub=sub, aps=aps, h=h: e.tensor_tensor(out=xt[:, xs + sub, h * 512:(h + 1) * 512], in0=aps,
                                                                              in1=xt[:, xs + sub, h * 512:(h + 1) * 512], op=ALU.add),
                     r=[ak, xt_k[xs + sub]], w=[xt_k[xs + sub]])

        def ln2_tail():
            assert state["ln"] == "2"
            for sub in range(NS):
                ln_stats(xs + sub)
            for sub in range(NS):
                ln_affine(xs + sub)
                R.dma("pool", ydst[row0 + sub * 128:row0 + (sub + 1) * 128, :], xt[:, xs + sub, :], sem="y%d" % (xs + sub),
                      r=[xt_k[xs + sub]])
        return ln2_tail

    tiles = []
    for t in range(n_pre):
        tiles.append(dict(kind="pre", xsrc=x_pre, psrc=None, row0=t * TT, NS=4, last=(t == n_pre - 1)))
    for t in range(n_main):
        tiles.append(dict(kind="main", xsrc=x_main, psrc=p_main, row0=t * TT, NS=4, ydst=y_main, ntok=TT, CS=64, nseq=1,
                          first=(t == 0), last=(t == n_main - 1)))
    if do_sample:
        tiles.append(dict(kind="sample", xsrc=x_s, psrc=p_s, row0=0, NS=1, ydst=y_s, ntok=128, CS=32, nseq=4, first=False, last=False))
    for i, T in enumerate(tiles):
        T["xs"] = (i % 2) * 4

    def prefetch_load(i):
        if i < len(tiles):
            T = tiles[i]
            tile_load(T["xsrc"], T["psrc"], T["row0"], T["NS"], T["xs"])

    def prefetch_front(i, eng="pool"):
        if i < len(tiles):
            T = tiles[i]
            front_A(T["NS"], T["xs"], eng)

    def pre_tile(i, T):
        xs = T["xs"]
        if not T.get("fb_done"):
            front_B(4, xs, False)
        prefetch_load(i + 1)
        proj_f("f0", 0, TT, 64, 4, False)
        proj_f("f1", 1, TT, 64, 4, False)
        prefetch_front(i + 1, "dve")
        proj_i("i0", 0, 4)
        proj_i("i1", 1, 4)
        if T["last"]:
            slab = next_slab("v")
            for g in range(4):
                ps_ap, pk = fm_proj(slab, g * 128, xT, xT_k[0:4], TT)
                R.op("act", lambda e, g=g, ps_ap=ps_ap: e.activation(out=carry[:, g, :], in_=ps_ap[:, TT - 15:TT], func=AF.Copy),
                     r=[pk], w=[carry_k[g]])
        if i + 1 < len(tiles) and tiles[i + 1]["kind"] == "pre":
            front_B(4, tiles[i + 1]["xs"], False)
            tiles[i + 1]["fb_done"] = True
        for sub in range(4):
            res = hgrn_stage1(sub, 64, False)
            for c in range(2):
                state_update_all(0, sub * 2 + c, res["dS"][c])
        cast_some(5)
        if T["last"]:
            for h in range(8):
                R.op("dve", lambda e, h=h: e.tensor_scalar(out=Sst[:, 0, h, :], in0=Sst[:, 0, h, :], scalar1=metat[:, 0:1], scalar2=None,
                                                           op0=ALU.mult), r=[Sst_k[0][h], meta_k], w=[Sst_k[0][h]])
            for g in range(4):
                R.op("dve", lambda e, g=g: e.tensor_scalar(out=carry[:, g, :], in0=carry[:, g, :], scalar1=metat[:, 0:1], scalar2=None,
                                                           op0=ALU.mult), r=[carry_k[g], meta_k], w=[carry_k[g]])
            cast_some(NSLAB)

    def main_tile(i, T):
        xs = T["xs"]; ntok = T["ntok"]; CS = T["CS"]; nseq = T["nseq"]; NS = T["NS"]
        sample = (T["kind"] == "sample")
        NC2 = 128 // CS
        if not T.get("fb_done"):
            front_B(NS, xs, True)
        proj_f("f0", 0, ntok, CS, NS, True)
        proj_q("q0", 0, ntok, NS)
        proj_f("f1", 1, ntok, CS, NS, True)
        proj_q("q1", 1, ntok, NS)
        proj_i("i0", 0, NS)
        proj_i("i1", 1, NS)
        if state.get("pending_ln2") is not None:
            state["pending_ln2"]()
            state["pending_ln2"] = None
        prefetch_load(i + 1)
        pool_dst = None
        if sample:
            pool_dst = [pools_out[j] for j in range(4)]
        elif T["last"]:
            pool_dst = [pool_out]
        if not sample:
            sbi = state.get("sb_last")
            if sbi is None:
                sbi = cast_state(0)
            sbls = []

            def chain_sub(sub, sbi):
                res = hgrn_stage1(sub, CS, False)
                sbl = [sbi]
                for c in range(NC2):
                    state_update_all(0, sub * NC2 + c, res["dS"][c])
                    sbi = cast_state(0)
                    if c < NC2 - 1:
                        sbl.append(sbi)
                sbls.append(sbl)
                return sbi

            sbi = chain_sub(0, sbi)
            proj_g("g0", 0, ntok, NS)
            sbi = chain_sub(1, sbi)
            proj_g("g1", 1, ntok, NS)
            sbi = chain_sub(2, sbi)
            hgrn_A(0, CS)
            hgrn_A(1, CS)
            proj_vpool(ntok, nseq, NS, sample, T["first"], True, pool_dst)
            hgrn_oA(0, CS, sbls[0])
            sbi = chain_sub(3, sbi)
            state["sb_last"] = sbi
            if T["last"]:
                R.dma("act", S_out.rearrange("h k v -> k h v"), Sst[:, 0, :, :], sem="sout", r=Sst_k[0])
            hgrn_A(2, CS)
            hgrn_oA(1, CS, sbls[1])
            hgrn_oB(0)
            hgrn_A(3, CS)
            hgrn_oA(2, CS, sbls[2])
            hgrn_oB(1)
            hgrn_oA(3, CS, sbls[3])
            hgrn_oB(2)
            hgrn_oB(3)
            pm_proj(ntok)
        else:
            proj_g("g0", 0, ntok, NS)
            proj_g("g1", 1, ntok, NS)
            proj_vpool(ntok, nseq, NS, sample, T["first"], False, pool_dst)
            pm_proj(ntok)
            res = hgrn_stage1(0, CS, True)
            sbl = []
            for j in range(4):
                a = j % 2
                R.dma("pool", Sst[:, a, :, :], s_hgrn[j].rearrange("h k v -> k h v"), sem="sl%d" % a, w=Sst_k[a])
                sbl.append(cast_state(a))
                state_update_all(a, j, res["dS"][j])
                R.dma("act", Ss_out[j].rearrange("h k v -> k h v"), Sst[:, a, :, :], sem="so%d" % a, r=Sst_k[a])
            hgrn_out(0, CS, sbl)
        merge_phase(ntok, NS, xs)
        prefetch_front(i + 1)
        def hook():
            if i + 1 < len(tiles):
                Tn = tiles[i + 1]
                front_B(Tn["NS"], Tn["xs"], True)
                Tn["fb_done"] = True
        tail = ffn_phase(ntok, NS, T["ydst"], T["row0"], xs, mid_hook=hook)
        if i + 1 < len(tiles):
            state["pending_ln2"] = tail
        else:
            tail()

    load_ln("in")
    cast_some(5 if n_pre > 0 else NSLAB)
    prefetch_load(0)
    prefetch_front(0, "dve" if n_pre > 0 else "pool")
    for i, T in enumerate(tiles):
        if T["kind"] == "pre":
            pre_tile(i, T)
        else:
            main_tile(i, T)

    for key in sorted(R.isdma):
        if key.startswith("y") or key.startswith("so") or key == "po":
            R._wait("act", key, R.cnt[key])

    assert slab_pos["i"] == len(slab_seq), (slab_pos["i"], len(slab_seq))
    with nc.Block() as block:
        @block.tensor
        def _(e):
            for f in R.ops["pe"]:
                f(e)

        @block.scalar
        def _(e):
            for f in R.ops["act"]:
                f(e)

        @block.vector
        def _(e):
            for f in R.ops["dve"]:
                f(e)

        @block.gpsimd
        def _(e):
            for f in R.ops["pool"]:
                f(e)

        @block.sync
        def _(e):
            for f in R.ops["sp"]:
                f(e)
    stack.close()
    return nc, {e: len(R.ops[e]) for e in R.ENG}


_CACHE = {}


def make_in_maps(inp, n_pre=8, n_main=8):
    f = lambda a: np.ascontiguousarray(np.asarray(a, dtype=np.float32))
    x_prompt = f(inp["x_prompt"]); x_sample = f(inp["x_sample"]); p_prompt = f(inp["p_prompt"]); p_sample = f(inp["p_sample"])
    state_hgrn = f(inp["state_hgrn"]); state_pool = f(inp["state_pool"])
    seg = n_main * TT
    npre = max(n_pre, 1) * TT
    shared = {
        "ln_in_g": f(inp["ln_in_g"]), "ln_in_b": f(inp["ln_in_b"]), "lb_logits": f(inp["lb_logits"]), "w_in": f(inp["w_in"])[0],
        "hg_g": f(inp["hgrn_norm_g"])[0], "w_a": f(inp["w_branch_a"])[0], "w_pm": f(inp["w_pool_mix"])[0],
        "p_scale": f(inp["pool_scale"])[0], "w_b": f(inp["w_branch_b"])[0], "w_o": f(inp["w_out"])[0],
        "ln1_g": f(inp["ln1_g"])[0], "ln1_b": f(inp["ln1_b"])[0], "w_up": f(inp["w_ffn_up"])[0],
        "w_down": f(inp["w_ffn_down"])[0], "w_pp": f(inp["w_ple_proj"])[0], "w_pg": f(inp["w_ple_gate"])[0],
        "ln2_g": f(inp["ln2_g"])[0], "ln2_b": f(inp["ln2_b"])[0],
    }
    in_maps = []
    for c in range(NCORE):
        j, half = c // 2, c % 2
        m = dict(shared)
        m["x_main"] = np.ascontiguousarray(x_prompt[j, half * seg:(half + 1) * seg])
        m["x_pre"] = np.ascontiguousarray(x_prompt[j, 0:npre])
        m["p_main"] = np.ascontiguousarray(p_prompt[0, j, half * seg:(half + 1) * seg])
        m["x_s"] = np.ascontiguousarray(x_sample[4 * c:4 * c + 4].reshape(128, D))
        m["p_s"] = np.ascontiguousarray(p_sample[0, 4 * c:4 * c + 4].reshape(128, 256))
        m["s_hgrn"] = np.ascontiguousarray(state_hgrn[0, 4 * c:4 * c + 4])
        m["s_pool"] = np.ascontiguousarray(state_pool[0, 4 * c:4 * c + 4])
        meta = np.zeros((128, 2), np.float32)
        meta[:, 0] = float(half)
        meta[:, 1] = float(half * seg)
        m["meta"] = meta
        in_maps.append(m)
    return in_maps


def gather(rs, n_main=8):
    seg = n_main * TT
    y_prompt = np.zeros((4, 2 * seg, D), np.float32)
    y_sample = np.zeros((32, 32, D), np.float32)
    hs_p = np.zeros((1, 4, 8, 128, 128), np.float32)
    pl_p = np.zeros((1, 4, 15, 512), np.float32)
    hs_s = np.zeros((1, 32, 8, 128, 128), np.float32)
    pl_s = np.zeros((1, 32, 15, 512), np.float32)
    for c in range(NCORE):
        j, half = c // 2, c % 2
        y_prompt[j, half * seg:(half + 1) * seg] = rs[c]["y_main"]
        y_sample[4 * c:4 * c + 4] = rs[c]["y_s"].reshape(4, 32, D)
        hs_s[0, 4 * c:4 * c + 4] = rs[c]["Ss_out"]
        pl_s[0, 4 * c:4 * c + 4] = rs[c]["pools_out"]
        if half == 1:
            hs_p[0, j] = rs[c]["S_out"]
            pl_p[0, j] = rs[c]["pool_out"]
    return (y_prompt, y_sample, hs_p, pl_p, hs_s, pl_s)


def kernel(**inputs):
    if "nc" not in _CACHE:
        _CACHE["nc"] = build_program()[0]
    nc = _CACHE["nc"]
    in_maps = make_in_maps(inputs)
    res = run_bass_kernel_spmd(nc, in_maps, core_ids=list(range(NCORE)))
    return gather(res.results)
```
